# Optimizing a Trainium2 kernel written in Bass

```python
import jax, jax.numpy as jnp
from jax import lax
import numpy as np

D_MODEL = 2048
BATCH = 4
SEQ = 2048
DEPTH = 4

GRID_W = 64
CTX_LEN = 256
N_MIXERS = 3
EPS = 1e-6
Q_BLOCK = 128
ROPE_THETA = 10000.0
NA_HEAD_DIM = 128
NA_HEADS = D_MODEL // NA_HEAD_DIM
WIN_H = 8
WIN_W = 16
MLA_HEADS = D_MODEL // 128
MLA_NOPE = 128
MLA_ROPE = 64
MLA_V = 128
MLA_Q_RANK = 512
MLA_KV_RANK = 512
CHUNK = 128
SG_WIDTH = D_MODEL
SG_GROUP_DIM = 128
SG_GROUPS = SG_WIDTH // SG_GROUP_DIM
D_FF = ((8 * D_MODEL // 3 + 255) // 256) * 256
CONV_W = 3

kernel_name = "hybrid_natten_mla_sgu_convffn_prefix_ctx"


def rmsnorm(z, g):
    zf = z.astype(jnp.float32)
    y = zf * lax.rsqrt(jnp.mean(zf * zf, axis=-1, keepdims=True) + EPS)
    return y.astype(z.dtype) * g


def layernorm(z, g, b):
    zf = z.astype(jnp.float32)
    mu = jnp.mean(zf, axis=-1, keepdims=True)
    var = jnp.mean(jnp.square(zf - mu), axis=-1, keepdims=True)
    return ((zf - mu) * lax.rsqrt(var + EPS)).astype(z.dtype) * g + b


def modulate(z, g, shift, scale):
    return rmsnorm(z, g) * (1 + scale) + shift


def merge_heads(o):
    B, H, N, dh = o.shape
    return o.transpose(0, 2, 1, 3).reshape(B, N, H * dh)


def dense_attention(q, k, v, scale):
    s = jnp.einsum('bhqd,bhkd->bhqk', q, k) * scale
    p = jax.nn.softmax(s.astype(jnp.float32), axis=-1).astype(v.dtype)
    return jnp.einsum('bhqk,bhkd->bhqd', p, v)


def apply_axial_rope(z):
    n, dim = z.shape[-2], z.shape[-1]
    half = dim // 2
    freqs = ROPE_THETA ** (-jnp.arange(0, half, 2, dtype=jnp.float32) / half)
    t = jnp.arange(n)
    rows = (t // GRID_W).astype(jnp.float32)
    cols = (t % GRID_W).astype(jnp.float32)

    def rot(u, ang):
        cs, sn = jnp.cos(ang).astype(u.dtype), jnp.sin(ang).astype(u.dtype)
        u1, u2 = jnp.split(u, 2, axis=-1)
        return jnp.concatenate([u1 * cs - u2 * sn, u1 * sn + u2 * cs], axis=-1)

    return jnp.concatenate([rot(z[..., :half], rows[:, None] * freqs),
                            rot(z[..., half:], cols[:, None] * freqs)], axis=-1)


def neighborhood_attention(h, hc, w_qkv, w_o, rpb, need_ctx_out):
    B, S, D = h.shape
    rows = S // GRID_W
    kh, kw = min(WIN_H, rows), WIN_W
    H, dh = NA_HEADS, NA_HEAD_DIM
    scale = dh ** -0.5

    def heads(z):
        n = z.shape[1]
        qkv = (z @ w_qkv).reshape(B, n, 3, H, dh).transpose(2, 0, 3, 1, 4)
        return qkv[0], qkv[1], qkv[2]

    q, k, v = heads(h)
    qc, kc, vc = heads(hc)
    q, k, v = (t.reshape(B, H, rows, GRID_W, dh) for t in (q, k, v))
    r_ar, w_ar = np.arange(rows), np.arange(GRID_W)
    row_idx = np.clip(r_ar - kh // 2, 0, rows - kh)[:, None] + np.arange(kh)
    col_idx = np.clip(w_ar - kw // 2, 0, GRID_W - kw)[:, None] + np.arange(kw)
    k_band = k[:, :, row_idx]
    v_band = v[:, :, row_idx]
    col_sel = jnp.asarray(np.eye(GRID_W)[col_idx], h.dtype)
    s_band = jnp.einsum('bhrqd,bhrakd->bhrqak', q, k_band)
    s_win = jnp.einsum('bhrqak,qjk->bhrqaj', s_band, col_sel)
    row_off = row_idx - r_ar[:, None] + WIN_H - 1
    col_off = col_idx - w_ar[:, None] + WIN_W - 1
    bias = rpb[:, row_off[:, None, :, None], col_off[None, :, None, :]]
    s_win = s_win * scale + bias
    s_ctx = jnp.einsum('bhrqd,bhcd->bhrqc', q, kc) * scale
    logits = jnp.concatenate([s_win.reshape(B, H, rows, GRID_W, kh * kw), s_ctx], axis=-1)
    p = jax.nn.softmax(logits.astype(jnp.float32), axis=-1).astype(v.dtype)
    p_win = p[..., :kh * kw].reshape(B, H, rows, GRID_W, kh, kw)
    p_band = jnp.einsum('bhrqaj,qjk->bhrqak', p_win, col_sel)
    o = (jnp.einsum('bhrqak,bhrakd->bhrqd', p_band, v_band)
         + jnp.einsum('bhrqc,bhcd->bhrqd', p[..., kh * kw:], vc))
    y = o.transpose(0, 2, 3, 1, 4).reshape(B, S, H * dh) @ w_o
    yc = merge_heads(dense_attention(qc, kc, vc, scale)) @ w_o if need_ctx_out else None
    return y, yc


def mla_attend(qn, qp, kn, kpe, v, scale):
    s = (jnp.einsum('bhqd,bhkd->bhqk', qn, kn) + jnp.einsum('bhqr,bkr->bhqk', qp, kpe)) * scale
    p = jax.nn.softmax(s.astype(jnp.float32), axis=-1).astype(v.dtype)
    return jnp.einsum('bhqk,bhkd->bhqd', p, v)


def latent_attention(h, hc, w_in, q_norm, kv_norm, w_uq, w_ukv, w_o, need_ctx_out):
    H = MLA_HEADS
    scale = (MLA_NOPE + MLA_ROPE) ** -0.5

    def project(z, rope):
        B, n, _ = z.shape
        cq, ckv, kpe = jnp.split(z @ w_in, [MLA_Q_RANK, MLA_Q_RANK + MLA_KV_RANK], axis=-1)
        q = (rmsnorm(cq, q_norm) @ w_uq).reshape(B, n, H, MLA_NOPE + MLA_ROPE).transpose(0, 2, 1, 3)
        kv = (rmsnorm(ckv, kv_norm) @ w_ukv).reshape(B, n, H, MLA_NOPE + MLA_V).transpose(0, 2, 1, 3)
        qn, qp = q[..., :MLA_NOPE], q[..., MLA_NOPE:]
        kn, v = kv[..., :MLA_NOPE], kv[..., MLA_NOPE:]
        if rope:
            qp, kpe = apply_axial_rope(qp), apply_axial_rope(kpe)
        return qn, qp, kn, kpe, v

    qn, qp, kn, kpe, v = project(h, True)
    qnc, qpc, knc, kpec, vc = project(hc, False)
    kn_all = jnp.concatenate([kn, knc], axis=2)
    kpe_all = jnp.concatenate([kpe, kpec], axis=1)
    v_all = jnp.concatenate([v, vc], axis=2)
    B, _, S, _ = qn.shape
    nb = S // Q_BLOCK

    def blocks(t):
        return t.reshape(B, H, nb, Q_BLOCK, t.shape[-1]).transpose(2, 0, 1, 3, 4)

    o = lax.map(lambda qs: mla_attend(qs[0], qs[1], kn_all, kpe_all, v_all, scale), (blocks(qn), blocks(qp)))
    o = o.transpose(1, 2, 0, 3, 4).reshape(B, H, S, MLA_V)
    y = merge_heads(o) @ w_o
    yc = merge_heads(mla_attend(qnc, qpc, knc, kpec, vc, scale)) @ w_o if need_ctx_out else None
    return y, yc


def spatial_gating(z, w_in, b_in, ln_g, ln_b, ws, bs, w_o):
    B, n, _ = z.shape
    u, v = jnp.split(jax.nn.gelu(z @ w_in + b_in, approximate=False), 2, axis=-1)
    v = layernorm(v, ln_g, ln_b).reshape(B, n // CHUNK, CHUNK, SG_GROUPS, SG_GROUP_DIM)
    mix = jnp.einsum('gpk,bnkgd->bnpgd', ws, v) + bs.T[:, :, None]
    return (u * mix.reshape(B, n, SG_WIDTH)) @ w_o


def conv_ffn(z, w_in, conv_w, conv_b, w_out):
    n = z.shape[1]
    u = z @ w_in
    pad = CONV_W // 2
    up = jnp.pad(u, ((0, 0), (pad, pad), (0, 0)))
    u = sum(up[:, t:t + n] * conv_w[t] for t in range(CONV_W)) + conv_b
    a, g = jnp.split(u, 2, axis=-1)
    return (a * jax.nn.silu(g)) @ w_out


def _normal(key, shape, scale):
    return jax.random.normal(key, shape, jnp.float32) * scale


def setup_inputs(seed: int = 0) -> dict:
    key = jax.random.key(seed)
    ks = iter(jax.random.split(key, 32))
    D = D_MODEL
    nA, nB, nC = (len(range(kind, DEPTH, N_MIXERS)) for kind in range(N_MIXERS))
    return {
        "x": _normal(next(ks), (BATCH, SEQ, D), 1.0),
        "c": _normal(next(ks), (BATCH, D), 1.0),
        "ctx": _normal(next(ks), (BATCH, CTX_LEN, D), 1.0),
        "c_ctx": _normal(next(ks), (D,), 1.0),
        "ada_w": _normal(next(ks), (DEPTH, D, 6 * D), 0.5 * D ** -0.5),
        "ada_b": _normal(next(ks), (DEPTH, 6 * D), 0.02),
        "norm_g": 1.0 + _normal(next(ks), (DEPTH, 2, D), 0.02),
        "final_g": 1.0 + _normal(next(ks), (D,), 0.02),
        "a_w_qkv": _normal(next(ks), (nA, D, 3 * NA_HEADS * NA_HEAD_DIM), D ** -0.5),
        "a_w_o": _normal(next(ks), (nA, NA_HEADS * NA_HEAD_DIM, D), (NA_HEADS * NA_HEAD_DIM) ** -0.5),
        "a_rpb": _normal(next(ks), (nA, NA_HEADS, 2 * WIN_H - 1, 2 * WIN_W - 1), 0.1),
        "b_w_in": _normal(next(ks), (nB, D, MLA_Q_RANK + MLA_KV_RANK + MLA_ROPE), D ** -0.5),
        "b_q_norm": 1.0 + _normal(next(ks), (nB, MLA_Q_RANK), 0.02),
        "b_kv_norm": 1.0 + _normal(next(ks), (nB, MLA_KV_RANK), 0.02),
        "b_w_uq": _normal(next(ks), (nB, MLA_Q_RANK, MLA_HEADS * (MLA_NOPE + MLA_ROPE)), MLA_Q_RANK ** -0.5),
        "b_w_ukv": _normal(next(ks), (nB, MLA_KV_RANK, MLA_HEADS * (MLA_NOPE + MLA_V)), MLA_KV_RANK ** -0.5),
        "b_w_o": _normal(next(ks), (nB, MLA_HEADS * MLA_V, D), (MLA_HEADS * MLA_V) ** -0.5),
        "c_w_in": _normal(next(ks), (nC, D, 2 * SG_WIDTH), D ** -0.5),
        "c_b_in": _normal(next(ks), (nC, 2 * SG_WIDTH), 0.02),
        "c_ln_g": 1.0 + _normal(next(ks), (nC, SG_WIDTH), 0.02),
        "c_ln_b": _normal(next(ks), (nC, SG_WIDTH), 0.02),
        "c_ws": _normal(next(ks), (nC, SG_GROUPS, CHUNK, CHUNK), CHUNK ** -0.5),
        "c_bs": 1.0 + _normal(next(ks), (nC, SG_GROUPS, CHUNK), 0.02),
        "c_w_o": _normal(next(ks), (nC, SG_WIDTH, D), SG_WIDTH ** -0.5),
        "f_w_in": _normal(next(ks), (DEPTH, D, 2 * D_FF), D ** -0.5),
        "f_conv_w": _normal(next(ks), (DEPTH, CONV_W, 2 * D_FF), CONV_W ** -0.5),
        "f_conv_b": _normal(next(ks), (DEPTH, 2 * D_FF), 0.02),
        "f_w_out": _normal(next(ks), (DEPTH, D_FF, D), D_FF ** -0.5),
    }


def reference(x, c, ctx, c_ctx, ada_w, ada_b, norm_g, final_g,
              a_w_qkv, a_w_o, a_rpb,
              b_w_in, b_q_norm, b_kv_norm, b_w_uq, b_w_ukv, b_w_o,
              c_w_in, c_b_in, c_ln_g, c_ln_b, c_ws, c_bs, c_w_o,
              f_w_in, f_conv_w, f_conv_b, f_w_out):
    s_lat = jax.nn.silu(c)
    s_ctx = jax.nn.silu(c_ctx)
    xc = ctx
    for i in range(DEPTH):
        kind, j = i % N_MIXERS, i // N_MIXERS
        last = i == DEPTH - 1
        m = [t[:, None, :] for t in jnp.split(s_lat @ ada_w[i] + ada_b[i], 6, axis=-1)]
        mc = jnp.split(s_ctx @ ada_w[i] + ada_b[i], 6, axis=-1)
        h = modulate(x, norm_g[i, 0], m[0], m[1])
        hc = modulate(xc, norm_g[i, 0], mc[0], mc[1]) if (not last or kind != 2) else None
        if kind == 0:
            y, yc = neighborhood_attention(h, hc, a_w_qkv[j], a_w_o[j], a_rpb[j], not last)
        elif kind == 1:
            y, yc = latent_attention(h, hc, b_w_in[j], b_q_norm[j], b_kv_norm[j], b_w_uq[j],
                                     b_w_ukv[j], b_w_o[j], not last)
        else:
            sg = (c_w_in[j], c_b_in[j], c_ln_g[j], c_ln_b[j], c_ws[j], c_bs[j], c_w_o[j])
            y = spatial_gating(h, *sg)
            yc = spatial_gating(hc, *sg) if not last else None
        x = x + m[2] * y
        h = modulate(x, norm_g[i, 1], m[3], m[4])
        x = x + m[5] * conv_ffn(h, f_w_in[i], f_conv_w[i], f_conv_b[i], f_w_out[i])
        if not last:
            xc = xc + mc[2] * yc
            hc = modulate(xc, norm_g[i, 1], mc[3], mc[4])
            xc = xc + mc[5] * conv_ffn(hc, f_w_in[i], f_conv_w[i], f_conv_b[i], f_w_out[i])
    return rmsnorm(x, final_g)
```

```python
import os
import numpy as np
import concourse.bass as bass
import concourse.mybir as mybir
from concourse.bass_utils import run_bass_kernel_spmd

F32 = mybir.dt.float32
BF16 = mybir.dt.bfloat16
AF = mybir.ActivationFunctionType
ALU = mybir.AluOpType
AX = mybir.AxisListType

D = 2048
NL = 2048
NCX = 256
NT = NL + NCX
DEPTH = 4
DFF = 5632
EPS = 1e-6
TT = [(0, 512), (512, 512), (1024, 512), (1536, 512), (2048, 256)]
SAME_ENGINE_SYNC = True


class Buf:
    __slots__ = ("name", "w", "r", "dsem", "dcnt")

    def __init__(self, name):
        self.name = name
        self.w = None
        self.r = []
        self.dsem = None
        self.dcnt = 0


class Sched:
    ENGS = ("pe", "act", "dve", "pool", "sp")

    def __init__(self, nc):
        self.nc = nc
        self.ops = {e: [] for e in self.ENGS}
        self.nsem = 0
        self.esem = {e: self._sem("eng_" + e) for e in self.ENGS}
        self.dbufs = []
        self.free_dsems = []
        self.pending = {e: [] for e in self.ENGS}
        self.rr = 0

    def _sem(self, name):
        self.nsem += 1
        return self.nc.alloc_semaphore(name=name)

    def _hazards(self, eng, reads, writes):
        evs = self.pending[eng]
        self.pending[eng] = []
        for b in reads:
            if b.w is not None:
                evs.append(b.w)
        for b in writes:
            if b.w is not None:
                evs.append(b.w)
            evs.extend(b.r)
        return evs

    def op(self, eng, fn, reads=(), writes=()):
        lst = self.ops[eng]
        ev = ("e", eng, len(lst))
        waits = self._hazards(eng, reads, writes)
        lst.append({"fn": fn, "waits": waits, "sig": False, "dma": None})
        for b in writes:
            b.w = ev
            b.r = []
        for b in reads:
            b.r = [x for x in b.r if not (x[0] == "e" and x[1] == eng)]
            b.r.append(ev)
        return ev

    def dma(self, eng, fn, sbuf, reads=(), writes=()):
        if sbuf.dsem is None:
            if self.free_dsems:
                sbuf.dsem = self.free_dsems.pop()
            else:
                sbuf.dsem = [self._sem("dsem%d" % self.nsem), 0]
            self.dbufs.append(sbuf)
        sbuf.dsem[1] += 16
        ev = ("d", sbuf.dsem[0], sbuf.dsem[1])
        waits = self._hazards(eng, reads, writes)
        self.ops[eng].append({"fn": fn, "waits": waits, "sig": False, "dma": sbuf.dsem[0]})
        for b in writes:
            b.w = ev
            b.r = []
        for b in reads:
            b.r.append(ev)
        return ev

    def barrier(self):
        evs = []
        for e in ("pe", "act", "dve", "pool"):
            lst = self.ops[e]
            for idx in range(len(lst) - 1, -1, -1):
                if lst[idx]["dma"] is None:
                    evs.append(("e", e, idx))
                    break
        for b in self.dbufs:
            evs.append(("d", b.dsem[0], b.dsem[1]))
            self.free_dsems.append(b.dsem)
            b.dsem = None
        self.dbufs = []
        for e in self.ENGS:
            self.pending[e] = self.pending[e] + evs

    def alt(self):
        self.rr ^= 1
        return "act" if self.rr else "dve"

    def emit(self, final_events=()):
        nc = self.nc

        def skip(w, e):
            return w[0] == "e" and w[1] == e and (e in ("pe", "sp") or not SAME_ENGINE_SYNC)

        for e in self.ENGS:
            for o in self.ops[e]:
                for w in o["waits"]:
                    if w[0] == "e" and not skip(w, e):
                        self.ops[w[1]][w[2]]["sig"] = True
        for w in final_events:
            if w[0] == "e":
                self.ops[w[1]][w[2]]["sig"] = True
        cnt = {}
        for e in self.ENGS:
            c = 0
            arr = []
            for o in self.ops[e]:
                if o["sig"]:
                    c += 1
                arr.append(c)
            cnt[e] = arr

        def resolve(w):
            if w[0] == "e":
                return self.esem[w[1]], cnt[w[1]][w[2]]
            return w[1], w[2]

        def run(e, engine):
            waited = {}
            for o in self.ops[e]:
                need = {}
                for w in o["waits"]:
                    if skip(w, e):
                        continue
                    s, v = resolve(w)
                    k = id(s)
                    if waited.get(k, 0) >= v:
                        continue
                    if k not in need or need[k][1] < v:
                        need[k] = (s, v)
                for k, (s, v) in need.items():
                    engine.wait_ge(s, v)
                    waited[k] = v
                ins = o["fn"](engine)
                if o["dma"] is not None:
                    ins.then_inc(o["dma"], 16)
                elif o["sig"]:
                    ins.then_inc(self.esem[e], 1)
            if e == "sp":
                for w in final_events:
                    s, v = resolve(w)
                    engine.wait_ge(s, v)

        with nc.Block() as block:
            @block.tensor
            def _(eng):
                run("pe", eng)

            @block.scalar
            def _(eng):
                run("act", eng)

            @block.vector
            def _(eng):
                run("dve", eng)

            @block.gpsimd
            def _(eng):
                run("pool", eng)

            @block.sync
            def _(eng):
                run("sp", eng)


class Arena:
    def __init__(self, nc, nbytes):
        self.n4 = nbytes // 4
        self.t = nc.alloc_sbuf_tensor("arena", [128, self.n4], F32)
        self.off = 0
        self.cnt = 0

    def reset(self, mark=0):
        self.off = mark

    def mark(self):
        return self.off

    def take(self, free_shape, dtype, npart=128):
        n = 1
        for s in free_shape:
            n *= s
        esz = 4 if dtype == F32 else 2
        nb = (n * esz + 31) // 32 * 32
        o4 = self.off
        self.off += nb // 4
        assert self.off <= self.n4, f"arena overflow {self.off * 4} > {self.n4 * 4}"
        v = self.t[0:npart, o4:o4 + nb // 4]
        if dtype != F32:
            v = v.bitcast(dtype)
        v = v[:, 0:n]
        if len(free_shape) == 2:
            v = v.rearrange("p (a b) -> p a b", b=free_shape[1])
        elif len(free_shape) == 3:
            v = v.rearrange("p (a b c) -> p a b c", b=free_shape[1], c=free_shape[2])
        return v


def _fm(vec, n):
    return np.ascontiguousarray(np.asarray(vec, np.float32).reshape(n, 128).T)


CONST_LAYOUT = {}


def _const_offsets():
    off = 0
    lay = {}
    for name, n in (("ada_b", 4 * 96), ("norm_g", 4 * 2 * 16), ("final_g", 16), ("conv_w", 4 * 88 * 3),
                    ("conv_b", 4 * 88), ("q_norm", 4), ("kv_norm", 4), ("b_u", 16)):
        lay[name] = (off, n)
        off += n
    return lay, off


CONST_LAYOUT, NCONST = _const_offsets()
ROW_LAYOUT = {"b_v": (0, 2048), "ln_g": (2048, 2048), "ln_b": (4096, 2048), "bs": (6144, 2048)}
NROW = 8192


def _rope_tables():
    half = 32
    freqs = (10000.0 ** (-np.arange(0, half, 2, dtype=np.float32) / half)).astype(np.float32)
    t = np.arange(NL)
    rows = (t // 64).astype(np.float32)
    cols = (t % 64).astype(np.float32)
    ang_r = rows[None, :] * freqs[:, None]
    ang_c = cols[None, :] * freqs[:, None]
    C = np.zeros((64, NL), np.float32)
    Sn = np.zeros((64, NL), np.float32)
    for base, ang in ((0, ang_r), (32, ang_c)):
        cs, sn = np.cos(ang).astype(np.float32), np.sin(ang).astype(np.float32)
        C[base:base + 16] = cs
        C[base + 16:base + 32] = cs
        Sn[base:base + 16] = -sn
        Sn[base + 16:base + 32] = sn
    T = np.zeros((128, NT), np.float32)
    T[0:64, 0:NL] = C
    T[0:64, NL:] = 1.0
    T[64:128, 0:NL] = Sn
    return T


_ROPE_PERM = np.concatenate([np.arange(16, 32), np.arange(0, 16), np.arange(48, 64), np.arange(32, 48)])


def _na_table(rpb):
    q = np.arange(64)
    k = np.arange(64)
    cs = np.clip(q - 8, 0, 48)
    inwin = (k[:, None] >= cs[None, :]) & (k[:, None] <= cs[None, :] + 15)
    coff = np.clip(k[:, None] - q[None, :] + 15, 0, 30)
    g = rpb[:, :, coff]
    g = np.where(inwin[None, None], g, np.float32(-30000.0)).astype(np.float32)
    tab = np.empty((2, 64, 16, 14, 64), np.float32)
    for a2 in range(2):
        tab[a2] = g[:, a2:a2 + 14].transpose(2, 0, 1, 3)
    return np.ascontiguousarray(tab.reshape(128, 16 * 14 * 64))


def host_layout(inp, b):
    f = lambda a: np.ascontiguousarray(np.asarray(a, np.float32))
    m = {}
    m["xT_in"] = np.ascontiguousarray(np.concatenate([np.asarray(inp["x"][b], np.float32), np.asarray(inp["ctx"][b], np.float32)], axis=0).T)
    cv = np.empty((128, 16, 2), np.float32)
    cv[:, :, 0] = _fm(inp["c"][b], 16)
    cv[:, :, 1] = _fm(inp["c_ctx"], 16)
    m["cvec"] = cv
    cst = np.zeros((128, NCONST), np.float32)

    def put(name, arr):
        o, n = CONST_LAYOUT[name]
        cst[:, o:o + n] = arr.reshape(128, n)

    put("ada_b", np.stack([_fm(inp["ada_b"][i], 96) for i in range(4)], axis=1))
    put("norm_g", np.stack([_fm(inp["norm_g"][i, j], 16) for i in range(4) for j in range(2)], axis=1))
    put("final_g", _fm(inp["final_g"], 16))
    cw = np.stack([np.stack([_fm(inp["f_conv_w"][i, t], 88) for t in range(3)], axis=2) for i in range(4)], axis=1)
    put("conv_w", cw)
    put("conv_b", np.stack([_fm(inp["f_conv_b"][i], 88) for i in range(4)], axis=1))
    put("q_norm", _fm(inp["b_q_norm"][0], 4))
    put("kv_norm", _fm(inp["b_kv_norm"][0], 4))
    put("b_u", _fm(inp["c_b_in"][0][:2048], 16))
    m["consts"] = cst
    rows = np.zeros((1, NROW), np.float32)
    rows[0, 0:2048] = inp["c_b_in"][0][2048:]
    rows[0, 2048:4096] = inp["c_ln_g"][0]
    rows[0, 4096:6144] = inp["c_ln_b"][0]
    rows[0, 6144:8192] = np.asarray(inp["c_bs"][0]).reshape(-1)
    m["rows"] = rows
    m["ident"] = np.eye(128, dtype=np.float32)
    m["rope"] = _rope_tables()
    fold = np.zeros((128, 128), np.float32)
    fold[np.arange(64), np.arange(64)] = 1.0
    fold[np.arange(64) + 64, np.arange(64)] = 1.0
    m["fold"] = fold
    m["ada_w"] = f(inp["ada_w"]).reshape(4 * 2048, 6 * D)
    m["a_w_qkv"] = f(inp["a_w_qkv"]).reshape(2 * 2048, 6144)
    m["a_w_o"] = f(inp["a_w_o"]).reshape(2 * 2048, 2048)
    m["na_tab"] = np.concatenate([_na_table(np.asarray(inp["a_rpb"][j], np.float32)) for j in range(2)], axis=0)
    w_in = np.asarray(inp["b_w_in"][0], np.float32)
    m["b_w_in"] = np.ascontiguousarray(np.concatenate([w_in, w_in[:, 1024:1088][:, _ROPE_PERM]], axis=1))
    w_uq = np.asarray(inp["b_w_uq"][0], np.float32).reshape(512, 16, 192)
    m["b_w_uq_n"] = np.ascontiguousarray(w_uq[:, :, :128].reshape(512, 2048))
    m["b_w_uq_r"] = np.ascontiguousarray(
        np.concatenate([w_uq[:, :, 128:], w_uq[:, :, 128:][:, :, _ROPE_PERM]], axis=2).reshape(512, 2048))
    w_ukv = np.asarray(inp["b_w_ukv"][0], np.float32).reshape(512, 16, 256)
    m["b_w_uk"] = np.ascontiguousarray(w_ukv[:, :, :128].reshape(512, 2048))
    m["b_w_uv"] = np.ascontiguousarray(w_ukv[:, :, 128:].reshape(512, 2048))
    m["b_w_o"] = f(inp["b_w_o"][0])
    m["c_w_in"] = f(inp["c_w_in"][0])
    m["c_wsT"] = np.ascontiguousarray(np.asarray(inp["c_ws"][0], np.float32).transpose(2, 0, 1).reshape(128, 2048))
    m["c_w_o"] = f(inp["c_w_o"][0])
    m["f_w_in"] = f(inp["f_w_in"]).reshape(4 * 2048, 2 * DFF)
    m["f_w_out"] = f(inp["f_w_out"]).reshape(4 * DFF, 2048)
    return m


INPUT_SHAPES = {
    "xT_in": [D, NT], "cvec": [128, 16, 2], "consts": [128, NCONST], "rows": [1, NROW], "ident": [128, 128],
    "rope": [128, NT], "fold": [128, 128], "ada_w": [4 * 2048, 6 * D], "a_w_qkv": [2 * 2048, 6144], "a_w_o": [2 * 2048, 2048],
    "na_tab": [2 * 128, 16 * 14 * 64], "b_w_in": [2048, 1152], "b_w_uq_n": [512, 2048], "b_w_uq_r": [512, 2048],
    "b_w_uk": [512, 2048], "b_w_uv": [512, 2048], "b_w_o": [2048, 2048], "c_w_in": [2048, 4096],
    "c_wsT": [128, 2048], "c_w_o": [2048, 2048], "f_w_in": [4 * 2048, 2 * DFF], "f_w_out": [4 * DFF, 2048],
}


class Builder:
    def __init__(self, plan, debug_out=None):
        self.plan = plan
        nc = self.nc = bass.Bass("TRN2", target_bir_lowering=False)
        self.S = Sched(nc)
        self.I = {k: nc.dram_tensor(k, shp, F32, kind="ExternalInput").ap() for k, shp in INPUT_SHAPES.items()}
        self.out = nc.dram_tensor("out", [D, NL], F32, kind="ExternalOutput").ap()
        self.xT = nc.dram_tensor("xT_s", [D, NT], F32).ap()
        self.x_src = self.I["xT_in"]
        self.qT = nc.dram_tensor("qT_s", [16, 128, NT], BF16).ap()
        self.kT = nc.dram_tensor("kT_s", [16, 128, NT], BF16).ap()
        self.qpT = nc.dram_tensor("qpT_s", [16, 128, NT], BF16).ap()
        self.vd = nc.dram_tensor("v_s", [NT, D], BF16).ap()
        self.uT = nc.dram_tensor("uT_s", [D, NT], BF16).ap()
        self.BxT = {(k, t): Buf(f"xT{k}_{t}") for k in range(16) for t in range(5)}
        self.Bq = [Buf(f"qd{h}") for h in range(16)]
        self.Bk = [Buf(f"kd{h}") for h in range(16)]
        self.Bqp = [Buf(f"qpd{h}") for h in range(16)]
        self.Bvd = Buf("vd")
        self.BuT = Buf("uTd")
        self.hT = nc.alloc_sbuf_tensor("hT", [128, 16, NT], BF16)
        self.Bh = {(k, t): Buf(f"h{k}_{t}") for k in range(16) for t in range(5)}
        self.cst = nc.alloc_sbuf_tensor("cst", [128, NCONST], F32)
        self.Bcst = Buf("cst")
        self.mod = nc.alloc_sbuf_tensor("mod", [128, 4, 96, 2], F32)
        self.Bmod = Buf("mod")
        self.ones = nc.alloc_sbuf_tensor("ones", [128, 128], BF16)
        self.Bones = Buf("ones")
        self.ident = nc.alloc_sbuf_tensor("identsb", [128, 128], F32)
        self.Bident = Buf("ident")
        self.ab = nc.alloc_sbuf_tensor("ab", [128, 3, 2, 16], F32)
        self.Bab = Buf("ab")
        self.pb = [nc.alloc_psum_tensor(f"pb{i}", [128, 512], F32) for i in range(8)]
        self.Bpb = [Buf(f"pb{i}") for i in range(8)]
        self.psi = 0
        rem = nc.sbuf_bytes_remaining
        self.A = Arena(nc, (rem - 2048) // 32 * 32)
        self.final_events = []
        self.debug_out = debug_out
        self.build()

    def cv(self, name):
        o, n = CONST_LAYOUT[name]
        return self.cst[:, o:o + n]

    def ps(self, lo=0, hi=8):
        i = lo + (self.psi % (hi - lo))
        self.psi += 1
        return self.pb[i], self.Bpb[i]

    def tt_of(self, t0):
        return min(t0 // 512, 4)

    def build(self):
        S = self.S
        I = self.I
        S.dma("sp", lambda e: e.dma_start(out=self.cst[:, :], in_=I["consts"]), self.Bcst, writes=[self.Bcst])
        S.dma("sp", lambda e: e.dma_start(out=self.ident[:, :], in_=I["ident"]), self.Bident, writes=[self.Bident])
        S.op("dve", lambda e: e.memset(self.ones[:, :], 1.0), writes=[self.Bones])
        for step in self.plan:
            kind = step[0]
            self.A.reset()
            self.psi = 0
            if kind == "ada":
                self.phase_ada()
            elif kind == "init":
                self.phase_init()
            elif kind == "norm":
                self.phase_modnorm(step[1], step[2])
            elif kind == "ffn":
                self.phase_ffn(step[1])
            elif kind == "na":
                self.phase_na(step[1], step[2])
            elif kind == "mla":
                self.phase_mla(step[1])
            elif kind == "sgu":
                self.phase_sgu(step[1])
            elif kind == "final":
                if not getattr(self, "dbg", False):
                    self.phase_final()
            elif kind == "dump_xT":
                self.phase_dump_xT()
            else:
                raise ValueError(kind)
            S.barrier()
        S.emit(final_events=self.final_events)

    def wstream_init(self, nbuf=3, kc=16, cb=512):
        self.wb = [self.A.take([kc, cb], BF16) for _ in range(nbuf)]
        self.Bwb = [Buf(f"wb{i}_{self.A.cnt}") for i in range(nbuf)]
        self.A.cnt += 1
        self.wbi = 0

    def wload(self, w2d, r0, kc, c0, cb):
        i = self.wbi % len(self.wb)
        self.wbi += 1
        wb, B = self.wb[i], self.Bwb[i]
        src = w2d[r0:r0 + kc * 128, c0:c0 + cb].rearrange("(k p) n -> p k n", p=128)
        self.S.dma("pool", lambda e: e.dma_start(out=wb[:, 0:kc, 0:cb], in_=src), B, writes=[B])
        return wb, B

    def phase_ada(self):
        S, A = self.S, self.A
        cvt = A.take([16, 2], F32)
        Bcv = Buf("cvec")
        sT = A.take([16, 2], BF16)
        BsT = Buf("sT")
        S.dma("sp", lambda e: e.dma_start(out=cvt, in_=self.I["cvec"]), Bcv, writes=[Bcv])
        S.op("act", lambda e: e.activation(out=sT, in_=cvt, func=AF.Silu), reads=[Bcv], writes=[BsT])
        self.wstream_init(nbuf=int(os.environ.get("ADA_NBUF", "2")))
        for i in range(int(os.environ.get('ADA_LAYERS', DEPTH))):
            pst, Bps = self.ps(0, 2)
            for blk in range(int(os.environ.get('ADA_BLKS', 24))):
                wb, Bw = self.wload(self.I["ada_w"], i * 2048, 16, blk * 512, 512)
                for oc in range(4):
                    j = blk * 4 + oc
                    for k in range(16):
                        S.op("pe", lambda e, wb=wb, oc=oc, k=k, j=j, pst=pst: e.matmul(
                            pst[:, 2 * j:2 * j + 2], lhsT=wb[:, k, oc * 128:(oc + 1) * 128], rhs=sT[:, k, :],
                            start=(k == 0), stop=(k == 15)), reads=[Bw, BsT], writes=[Bps])
            o, n = CONST_LAYOUT["ada_b"]
            bias = self.cst[:, o + i * 96:o + (i + 1) * 96].unsqueeze(2).broadcast_to([128, 96, 2])
            S.op("dve", lambda e, i=i, pst=pst, bias=bias: e.tensor_tensor(
                out=self.mod[:, i, :, :], in0=pst[:, 0:192].rearrange("p (j s) -> p j s", s=2), in1=bias, op=ALU.add),
                reads=[Bps, self.Bcst], writes=[self.Bmod])

    def set_ab(self, layer, sub):
        S = self.S
        o, n = CONST_LAYOUT["norm_g"]
        g = self.cst[:, o + (layer * 2 + sub) * 16:o + (layer * 2 + sub + 1) * 16].unsqueeze(1).broadcast_to([128, 2, 16])
        base = sub * 48
        shift = self.mod[:, layer, base:base + 16, :].rearrange("p c s -> p s c")
        scale = self.mod[:, layer, base + 16:base + 32, :].rearrange("p c s -> p s c")
        gate = self.mod[:, layer, base + 32:base + 48, :].rearrange("p c s -> p s c")
        S.op("dve", lambda e: e.scalar_tensor_tensor(out=self.ab[:, 0, :, :], in0=scale, scalar=1.0, in1=g,
                                                     op0=ALU.add, op1=ALU.mult),
             reads=[self.Bmod, self.Bcst], writes=[self.Bab])
        S.op("dve", lambda e: e.tensor_copy(out=self.ab[:, 1, :, :], in_=shift), reads=[self.Bmod], writes=[self.Bab])
        S.op("dve", lambda e: e.tensor_copy(out=self.ab[:, 2, :, :], in_=gate), reads=[self.Bmod], writes=[self.Bab])

    def phase_init(self):
        S, A = self.S, self.A
        xin = [A.take([D], F32) for _ in range(2)]
        Bxin = [Buf(f"xin{i}") for i in range(2)]
        xo = [A.take([16, 128], F32) for _ in range(2)]
        Bxo = [Buf(f"xo{i}") for i in range(2)]
        for n in range(NT // 128):
            b = n % 2
            S.dma("sp", lambda e, n=n, b=b: e.dma_start(out=xin[b], in_=self.I["x_tok"][n * 128:(n + 1) * 128, :]),
                  Bxin[b], writes=[Bxin[b]])
            for kb in range(4):
                pst, Bps = self.ps()
                for kk in range(4):
                    k = kb * 4 + kk
                    S.op("pe", lambda e, b=b, k=k, kk=kk, pst=pst: e.transpose(
                        out=pst[:, kk * 128:(kk + 1) * 128], in_=xin[b][:, k * 128:(k + 1) * 128], identity=self.ident[:, :]),
                        reads=[Bxin[b], self.Bident], writes=[Bps])
                eng = S.alt()
                dst = xo[b][:, kb * 4:(kb + 1) * 4, :]
                src = pst[:, :].rearrange("p (a b) -> p a b", b=128)
                if eng == "act":
                    S.op("act", lambda e, dst=dst, src=src: e.copy(out=dst, in_=src), reads=[Bps], writes=[Bxo[b]])
                else:
                    S.op("dve", lambda e, dst=dst, src=src: e.tensor_copy(out=dst, in_=src), reads=[Bps], writes=[Bxo[b]])
            t = self.tt_of(n * 128)
            S.dma("sp", lambda e, n=n, b=b: e.dma_start(
                out=self.xT[:, n * 128:(n + 1) * 128].rearrange("(k p) t -> p k t", p=128), in_=xo[b]),
                Bxo[b], reads=[Bxo[b]], writes=[self.BxT[(k, t)] for k in range(16)])

    def phase_modnorm(self, layer, sub, final=False):
        S, A = self.S, self.A
        if not final:
            self.set_ab(layer, sub)
        xsrc = self.x_src
        SW = 256
        xt = [A.take([16, SW], F32) for _ in range(4)]
        Bxt = [Buf(f"mxt{i}") for i in range(4)]
        sq = [A.take([16, SW], BF16) for _ in range(2)]
        Bsq = [Buf(f"msq{i}") for i in range(2)]
        rs = [A.take([SW], F32) for _ in range(2)]
        Brs = [Buf(f"mrs{i}") for i in range(2)]
        tmp = [A.take([SW], F32) for _ in range(4)]
        Btmp = [Buf(f"mtmp{i}") for i in range(4)]
        ti = 0
        for n, t0 in enumerate(range(0, NT, SW)):
            w = SW
            t = self.tt_of(t0)
            b = n % 4
            b2 = n % 2
            s = 1 if t0 >= NL else 0
            S.dma("sp", lambda e, b=b, t0=t0, w=w: e.dma_start(
                out=xt[b][:, :, 0:w], in_=xsrc[:, t0:t0 + w].rearrange("(k p) t -> p k t", p=128)),
                Bxt[b], reads=[self.BxT[(k, t)] for k in range(16)], writes=[Bxt[b]])
            S.op("act", lambda e, b=b, b2=b2, w=w: e.activation(out=sq[b2][:, 0:8, 0:w], in_=xt[b][:, 0:8, 0:w], func=AF.Square),
                 reads=[Bxt[b]], writes=[Bsq[b2]])
            S.op("dve", lambda e, b=b, b2=b2, w=w: e.tensor_tensor(out=sq[b2][:, 8:16, 0:w], in0=xt[b][:, 8:16, 0:w],
                                                                   in1=xt[b][:, 8:16, 0:w], op=ALU.mult),
                 reads=[Bxt[b]], writes=[Bsq[b2]])
            pst, Bps = self.ps()
            for k in range(16):
                S.op("pe", lambda e, b2=b2, k=k, w=w, pst=pst: e.matmul(pst[:, 0:w], lhsT=self.ones[:, :], rhs=sq[b2][:, k, 0:w],
                                                                       start=(k == 0), stop=(k == 15)),
                     reads=[Bsq[b2], self.Bones], writes=[Bps])
            S.op("act", lambda e, b2=b2, w=w, pst=pst: e.activation(out=rs[b2][:, 0:w], in_=pst[:, 0:w], func=AF.Ln,
                                                                    bias=self.epsb[:, 0:1], scale=1.0 / D),
                 reads=[Bps, self.Beps], writes=[Brs[b2]])
            S.op("act", lambda e, b2=b2, w=w: e.activation(out=rs[b2][:, 0:w], in_=rs[b2][:, 0:w], func=AF.Exp, scale=-0.5),
                 reads=[Brs[b2]], writes=[Brs[b2]])
            for k in range(16):
                tb = ti % 4
                ti += 1
                Acol = self.ab[:, 0, s, k:k + 1]
                Bcol = self.ab[:, 1, s, k:k + 1]
                S.op("dve", lambda e, b=b, b2=b2, k=k, w=w, tb=tb, Acol=Acol: e.scalar_tensor_tensor(
                    out=tmp[tb][:, 0:w], in0=xt[b][:, k, 0:w], scalar=Acol, in1=rs[b2][:, 0:w], op0=ALU.mult, op1=ALU.mult),
                    reads=[Bxt[b], Brs[b2], self.Bab], writes=[Btmp[tb]])
                S.op("act", lambda e, k=k, w=w, t0=t0, tb=tb, Bcol=Bcol: e.activation(
                    out=self.hT[:, k, t0:t0 + w], in_=tmp[tb][:, 0:w], func=AF.Identity, bias=Bcol, scale=1.0),
                    reads=[Btmp[tb], self.Bab], writes=[self.Bh[(k, t)]])

    def resid_epilogue_factory(self, nbuf=6):
        S, A = self.S, self.A
        xr = [A.take([512], F32) for _ in range(nbuf)]
        Bxr = [Buf(f"xr{i}_{A.cnt}") for i in range(nbuf)]
        A.cnt += 1
        PD = nbuf - 3
        st = {"tiles": [], "next": 0}
        xsrc = self.x_src

        def plan(tiles):
            st["tiles"] = list(tiles)
            st["next"] = 0

        def issue(n):
            k, t, t0, w = st["tiles"][n]
            b = n % nbuf
            S.dma("sp", lambda e: e.dma_start(out=xr[b][:, 0:w], in_=xsrc[k * 128:(k + 1) * 128, t0:t0 + w]),
                  Bxr[b], reads=[self.BxT[(k, t)]], writes=[Bxr[b]])

        def pre(idx):
            while st["next"] < min(len(st["tiles"]), idx + PD + 1):
                issue(st["next"])
                st["next"] += 1

        def post(idx, pst, Bps):
            k, t, t0, w = st["tiles"][idx]
            b = idx % nbuf
            s_ = 1 if t0 >= NL else 0
            gcol = self.ab[:, 2, s_, k:k + 1]
            S.op("dve", lambda e: e.scalar_tensor_tensor(out=xr[b][:, 0:w], in0=pst[:, 0:w], scalar=gcol, in1=xr[b][:, 0:w],
                                                         op0=ALU.mult, op1=ALU.add),
                 reads=[Bps, Bxr[b], self.Bab], writes=[Bxr[b]])
            S.dma("sp", lambda e: e.dma_start(out=self.xT[k * 128:(k + 1) * 128, t0:t0 + w], in_=xr[b][:, 0:w]),
                  Bxr[b], reads=[Bxr[b]], writes=[self.BxT[(k, t)]])

        return plan, pre, post

    def oproj(self, w2d, r0):
        S = self.S
        self.wstream_init()
        plan, pre, post = self.resid_epilogue_factory()
        plan([(blk * 4 + oc, t, t0, w) for blk in range(4) for oc in range(4) for t, (t0, w) in enumerate(TT)])
        idx = 0
        for blk in range(4):
            wb, Bw = self.wload(w2d, r0, 16, blk * 512, 512)
            for oc in range(4):
                for t, (t0, w) in enumerate(TT):
                    pre(idx)
                    pst, Bps = self.ps()
                    for k in range(16):
                        S.op("pe", lambda e, wb=wb, oc=oc, k=k, t0=t0, w=w, pst=pst: e.matmul(
                            pst[:, 0:w], lhsT=wb[:, k, oc * 128:(oc + 1) * 128], rhs=self.hT[:, k, t0:t0 + w],
                            start=(k == 0), stop=(k == 15)), reads=[Bw, self.Bh[(k, t)]], writes=[Bps])
                    post(idx, pst, Bps)
                    idx += 1
        self.x_src = self.xT

    def phase_ffn(self, layer):
        S, A = self.S, self.A
        self.phase_modnorm(layer, 1)
        S.barrier()
        A.reset()
        self.psi = 0
        GROUPS = [9, 9, 9, 9, 8]
        JQM = 9
        act = A.take([JQM, NT], BF16)
        Bact = {(j, t): Buf(f"act{j}_{t}") for j in range(JQM) for t in range(5)}
        self.wstream_init(nbuf=4, kc=16, cb=256)
        win_wb, win_B = self.wb, self.Bwb
        self.wstream_init(nbuf=2, kc=JQM, cb=512)
        wout_wb, wout_B = self.wb, self.Bwb
        wsi = {"in": 0, "out": 0}

        def use(which):
            if which == "in":
                self.wb, self.Bwb, self.wbi = win_wb, win_B, wsi["in"]
            else:
                self.wb, self.Bwb, self.wbi = wout_wb, wout_B, wsi["out"]

        def done(which):
            wsi[which] = self.wbi
        plan, pre, post = self.resid_epilogue_factory(nbuf=6)
        ctiles = []
        t0 = 0
        while t0 < NL:
            w = min(410, NL - t0)
            ctiles.append((t0, w, t0 > 0, t0 + w < NL))
            t0 += w
        ctiles.append((NL, NCX, False, False))
        UW = 412
        ub = [[A.take([UW], F32) for _ in range(2)] for _ in range(2)]
        Bub = [[Buf(f"ub{x}{i}") for i in range(2)] for x in range(2)]
        cva = [A.take([UW], F32) for _ in range(2)]
        Bcva = [Buf(f"cva{i}") for i in range(2)]
        cvg = [A.take([UW], F32) for _ in range(2)]
        Bcvg = [Buf(f"cvg{i}") for i in range(2)]
        sg = [A.take([UW], F32) for _ in range(2)]
        Bsg = [Buf(f"sg{i}") for i in range(2)]
        ocw, _ = CONST_LAYOUT["conv_w"]
        ocb, _ = CONST_LAYOUT["conv_b"]
        wi = self.I["f_w_in"]
        wo = self.I["f_w_out"]
        it = 0
        gstart = 0
        for q, JQ in enumerate(GROUPS):
            for jb in range(0, JQ, 2):
                nj = min(2, JQ - jb)
                ja0 = gstart + jb
                use("in")
                wa, Bwa = self.wload(wi, layer * 2048, 16, ja0 * 128, nj * 128)
                wg, Bwg = self.wload(wi, layer * 2048, 16, DFF + ja0 * 128, nj * 128)
                done("in")
                for jj in range(nj):
                    jl = jb + jj
                    ja = ja0 + jj
                    cw = lambda c, tap: self.cst[:, ocw + ((layer * 88 + c) * 3 + tap):ocw + ((layer * 88 + c) * 3 + tap) + 1]
                    cb = lambda c: self.cst[:, ocb + layer * 88 + c:ocb + layer * 88 + c + 1]
                    for (c0, w, hl, hr) in ctiles:
                        r = it % 2
                        it += 1
                        m0 = c0 - (1 if hl else 0)
                        mw = w + (1 if hl else 0) + (1 if hr else 0)
                        uo = 0 if hl else 1
                        tlist = sorted(set(self.tt_of(x) for x in (m0, m0 + mw - 1)))
                        hreads = lambda k: [self.Bh[(k, t)] for t in tlist]
                        pa, Bpa = self.ps()
                        for k in range(16):
                            S.op("pe", lambda e, wa=wa, jj=jj, k=k, m0=m0, mw=mw, pa=pa: e.matmul(
                                pa[:, 0:mw], lhsT=wa[:, k, jj * 128:(jj + 1) * 128], rhs=self.hT[:, k, m0:m0 + mw],
                                start=(k == 0), stop=(k == 15)), reads=[Bwa] + hreads(k), writes=[Bpa])
                        pg, Bpg = self.ps()
                        for k in range(16):
                            S.op("pe", lambda e, wg=wg, jj=jj, k=k, m0=m0, mw=mw, pg=pg: e.matmul(
                                pg[:, 0:mw], lhsT=wg[:, k, jj * 128:(jj + 1) * 128], rhs=self.hT[:, k, m0:m0 + mw],
                                start=(k == 0), stop=(k == 15)), reads=[Bwg] + hreads(k), writes=[Bpg])
                        for x, (pp, Bpp) in enumerate(((pa, Bpa), (pg, Bpg))):
                            u, Bu = ub[x][r], Bub[x][r]
                            S.op("act", lambda e, u=u, pp=pp, uo=uo, mw=mw: e.copy(out=u[:, uo:uo + mw], in_=pp[:, 0:mw]),
                                 reads=[Bpp], writes=[Bu])
                            if not hl:
                                S.op("act", lambda e, u=u: e.memzero(u[:, 0:1]), writes=[Bu])
                            if not hr:
                                S.op("act", lambda e, u=u, w=w: e.memzero(u[:, w + 1:w + 2]), writes=[Bu])
                        for x, (cvx, Bcvx, eng) in enumerate(((cva[r], Bcva[r], "dve"), (cvg[r], Bcvg[r], "dve"))):
                            u, Bu = ub[x][r], Bub[x][r]
                            c = ja if x == 0 else 44 + ja
                            S.op(eng, lambda e, cvx=cvx, u=u, w=w, c=c: e.tensor_scalar_mul(
                                out=cvx[:, 0:w], in0=u[:, 0:w], scalar1=cw(c, 0)),
                                reads=[Bu, self.Bcst], writes=[Bcvx])
                            S.op(eng, lambda e, cvx=cvx, u=u, w=w, c=c: e.scalar_tensor_tensor(
                                out=cvx[:, 0:w], in0=u[:, 1:w + 1], scalar=cw(c, 1), in1=cvx[:, 0:w], op0=ALU.mult, op1=ALU.add),
                                reads=[Bu, Bcvx, self.Bcst], writes=[Bcvx])
                            S.op(eng, lambda e, cvx=cvx, u=u, w=w, c=c: e.scalar_tensor_tensor(
                                out=cvx[:, 0:w], in0=u[:, 2:w + 2], scalar=cw(c, 2), in1=cvx[:, 0:w], op0=ALU.mult, op1=ALU.add),
                                reads=[Bu, Bcvx, self.Bcst], writes=[Bcvx])
                        S.op("act", lambda e, r=r, w=w, ja=ja: e.activation(out=sg[r][:, 0:w], in_=cvg[r][:, 0:w], func=AF.Silu,
                                                                            bias=cb(44 + ja), scale=1.0),
                             reads=[Bcvg[r], self.Bcst], writes=[Bsg[r]])
                        tl2 = sorted(set(self.tt_of(x) for x in (c0, c0 + w - 1)))
                        S.op("dve", lambda e, r=r, w=w, ja=ja, jl=jl, c0=c0: e.scalar_tensor_tensor(
                            out=act[:, jl, c0:c0 + w], in0=cva[r][:, 0:w], scalar=cb(ja), in1=sg[r][:, 0:w], op0=ALU.add, op1=ALU.mult),
                            reads=[Bcva[r], Bsg[r], self.Bcst], writes=[Bact[(jl, t)] for t in tl2])
            plan([(blk * 4 + oc, t, t0, w) for blk in range(4) for oc in range(4) for t, (t0, w) in enumerate(TT)])
            idx = 0
            for blk in range(4):
                use("out")
                wb, Bw = self.wload(wo, layer * DFF + gstart * 128, JQ, blk * 512, 512)
                done("out")
                for oc in range(4):
                    kout = blk * 4 + oc
                    for t, (t0, w) in enumerate(TT):
                        pre(idx)
                        pst, Bps = self.ps()
                        for j in range(JQ):
                            S.op("pe", lambda e, wb=wb, oc=oc, j=j, t0=t0, w=w, pst=pst: e.matmul(
                                pst[:, 0:w], lhsT=wb[:, j, oc * 128:(oc + 1) * 128], rhs=act[:, j, t0:t0 + w],
                                start=(j == 0), stop=(j == JQ - 1)), reads=[Bw, Bact[(j, t)]], writes=[Bps])
                        post(idx, pst, Bps)
                        idx += 1
            gstart += JQ

    def phase_final(self):
        S, A = self.S, self.A
        o, n = CONST_LAYOUT["final_g"]
        xt = [A.take([16, 512], F32) for _ in range(2)]
        Bxt = [Buf(f"fxt{i}") for i in range(2)]
        sq = [A.take([16, 512], BF16) for _ in range(2)]
        Bsq = [Buf(f"fsq{i}") for i in range(2)]
        rs = [A.take([512], F32) for _ in range(2)]
        Brs = [Buf(f"frs{i}") for i in range(2)]
        yo = [A.take([D], F32) for _ in range(2)]
        Byo = [Buf(f"yo{i}") for i in range(2)]
        BO = Buf("OUT")
        oi = 0
        for t, (t0, w) in enumerate(TT[:4]):
            b = t % 2
            S.dma("sp", lambda e, b=b, t0=t0, w=w: e.dma_start(
                out=xt[b][:, :, 0:w], in_=self.xT[:, t0:t0 + w].rearrange("(k p) t -> p k t", p=128)),
                Bxt[b], reads=[self.BxT[(k, t)] for k in range(16)], writes=[Bxt[b]])
            S.op("act", lambda e, b=b, w=w: e.activation(out=sq[b][:, :, 0:w], in_=xt[b][:, :, 0:w], func=AF.Square),
                 reads=[Bxt[b]], writes=[Bsq[b]])
            pst, Bps = self.ps()
            for k in range(16):
                S.op("pe", lambda e, b=b, k=k, w=w, pst=pst: e.matmul(pst[:, 0:w], lhsT=self.ones[:, :], rhs=sq[b][:, k, 0:w],
                                                                      start=(k == 0), stop=(k == 15)),
                     reads=[Bsq[b], self.Bones], writes=[Bps])
            S.op("act", lambda e, b=b, w=w, pst=pst: e.activation(out=rs[b][:, 0:w], in_=pst[:, 0:w], func=AF.Ln,
                                                                  bias=self.epsb[:, 0:1], scale=1.0 / D),
                 reads=[Bps, self.Beps], writes=[Brs[b]])
            S.op("act", lambda e, b=b, w=w: e.activation(out=rs[b][:, 0:w], in_=rs[b][:, 0:w], func=AF.Exp, scale=-0.5),
                 reads=[Brs[b]], writes=[Brs[b]])
            for k in range(16):
                gcol = self.cst[:, o + k:o + k + 1]
                S.op("dve", lambda e, b=b, k=k, w=w, gcol=gcol: e.scalar_tensor_tensor(
                    out=xt[b][:, k, 0:w], in0=xt[b][:, k, 0:w], scalar=gcol, in1=rs[b][:, 0:w], op0=ALU.mult, op1=ALU.mult),
                    reads=[Bxt[b], Brs[b], self.Bcst], writes=[Bxt[b]])
            ev = S.dma("sp", lambda e, b=b, t0=t0, w=w: e.dma_start(
                out=self.out[:, t0:t0 + w].rearrange("(k p) t -> p k t", p=128), in_=xt[b][:, :, 0:w]),
                Bxt[b], reads=[Bxt[b]], writes=[Buf(f"OUT{t}")])
            self.final_events.append(ev)

    def dbg_sb(self, src, Bsrc, r0, npart):
        S, A = self.S, self.A
        f = A.take([2048], F32)
        Bf = Buf(f"dbgf{A.cnt}")
        A.cnt += 1
        S.op("dve", lambda e: e.tensor_copy(out=f[0:npart, :], in_=src), reads=[Bsrc] if Bsrc else [], writes=[Bf])
        ev = S.dma("sp", lambda e: e.dma_start(out=self.out[r0:r0 + npart, :], in_=f[0:npart, :]), Bf, reads=[Bf], writes=[Buf("o")])
        self.final_events.append(ev)
        self.dbg = True

    def dbg_rows(self, src_dram, r0, npart, dtype):
        S, A = self.S, self.A
        b = A.take([2048], dtype)
        Bb = Buf(f"dbgb{A.cnt}")
        A.cnt += 1
        S.dma("sp", lambda e: e.dma_start(out=b[0:npart, :], in_=src_dram), Bb, writes=[Bb])
        self.dbg_sb(b[0:npart, :], Bb, r0, npart)

    def stager(self, n, width, dtype=BF16):
        A = self.A
        bufs = [A.take([width], dtype) for _ in range(n)]
        Bs = [Buf(f"stg{i}_{A.cnt}") for i in range(n)]
        A.cnt += 1
        st = {"i": 0}

        def nxt():
            i = st["i"] % n
            st["i"] += 1
            return bufs[i], Bs[i]
        return nxt

    def evac_store(self, nxt, src, Bsrc, dst, dstB, npart=128, func=None, bias=None, Bbias=None):
        S = self.S
        stg, Bstg = nxt()
        w = src.shape[-1]
        o = stg[0:npart, 0:w]
        if func is not None:
            S.op("act", lambda e: e.activation(out=o, in_=src, func=func, bias=bias, scale=1.0),
                 reads=[Bsrc] + ([Bbias] if Bbias else []), writes=[Bstg])
        elif S.alt() == "act":
            S.op("act", lambda e: e.copy(out=o, in_=src), reads=[Bsrc], writes=[Bstg])
        else:
            S.op("dve", lambda e: e.tensor_copy(out=o, in_=src), reads=[Bsrc], writes=[Bstg])
        S.dma("sp", lambda e: e.dma_start(out=dst, in_=o), Bstg, reads=[Bstg], writes=dstB)

    def proj_fm(self, wb, Bw, oc, kc, src, srcB, epi):
        S = self.S
        for t, (t0, w) in enumerate(TT):
            pst, Bps = self.ps()
            for k in range(kc):
                S.op("pe", lambda e, k=k, t0=t0, w=w, pst=pst: e.matmul(
                    pst[:, 0:w], lhsT=wb[:, k, oc * 128:(oc + 1) * 128], rhs=src[:, k, t0:t0 + w],
                    start=(k == 0), stop=(k == kc - 1)), reads=[Bw, srcB(k, t)], writes=[Bps])
            epi(pst, Bps, t, t0, w)

    def proj_tm(self, wb, Bw, cb, kc, src, srcB, epi):
        S = self.S
        for n in range(NT // 128):
            t = self.tt_of(n * 128)
            pst, Bps = self.ps()
            for k in range(kc):
                S.op("pe", lambda e, k=k, n=n, pst=pst: e.matmul(
                    pst[:, 0:cb], lhsT=src[:, k, n * 128:(n + 1) * 128], rhs=wb[:, k, 0:cb],
                    start=(k == 0), stop=(k == kc - 1)), reads=[Bw, srcB(k, t)], writes=[Bps])
            epi(pst, Bps, n)

    def attn_finalize(self, o_ps, Bo, s_ps, Bs, w, dst, dstB, rsn):
        S = self.S
        rs, Brs = rsn()
        S.op("dve", lambda e: e.reciprocal(out=rs[:, 0:w], in_=s_ps[:, 0:w]), reads=[Bs], writes=[Brs])
        S.op("dve", lambda e: e.tensor_tensor(out=dst, in0=o_ps[:, 0:w], in1=rs[:, 0:w], op=ALU.mult),
             reads=[Bo, Brs], writes=[dstB])

    def phase_na(self, layer, j):
        S, A = self.S, self.A
        hB = lambda k, t: self.Bh[(k, t)]
        Bq = {(h, t): Buf(f"qd{h}_{t}") for h in range(16) for t in range(5)}
        Bk = {(h, t): Buf(f"kd{h}_{t}") for h in range(16) for t in range(5)}
        Bv = [Buf(f"vd{n}") for n in range(18)]
        self.wstream_init()
        nxt = self.stager(3, 512)
        w2d = self.I["a_w_qkv"]
        r0 = j * 2048
        for blk in range(8):
            wb, Bw = self.wload(w2d, r0, 16, blk * 512, 512)
            for oc in range(4):
                c = blk * 4 + oc
                h = c % 16
                dst, Bd = (self.qT, Bq) if c < 16 else (self.kT, Bk)
                self.proj_fm(wb, Bw, oc, 16, self.hT, hB,
                             lambda pst, Bps, t, t0, w, h=h, dst=dst, Bd=Bd: self.evac_store(
                                 nxt, pst[:, 0:w], Bps, dst[h, :, t0:t0 + w], [Bd[(h, t)]]))
        for blk in range(4):
            wb, Bw = self.wload(w2d, r0, 16, 4096 + blk * 512, 512)
            self.proj_tm(wb, Bw, 512, 16, self.hT, hB,
                         lambda pst, Bps, n, blk=blk: self.evac_store(
                             nxt, pst[:, 0:512], Bps, self.vd[n * 128:(n + 1) * 128, blk * 512:(blk + 1) * 512], [Bv[n]]))
        S.barrier()
        A.reset()
        self.psi = 0
        tab = A.take([16, 14, 64], BF16)
        Btab = Buf("natab")
        tst = [A.take([896], F32) for _ in range(2)]
        Btst = [Buf(f"tst{i}") for i in range(2)]
        for h in range(16):
            b = h % 2
            S.dma("sp", lambda e, h=h, b=b: e.dma_start(out=tst[b], in_=self.I["na_tab"][j * 128:(j + 1) * 128, h * 896:(h + 1) * 896]),
                  Btst[b], writes=[Btst[b]])
            S.op("act", lambda e, h=h, b=b: e.activation(out=tab[:, h, :, :].rearrange("p a b -> p (a b)"), in_=tst[b], func=AF.Exp),
                 reads=[Btst[b]], writes=[Btab])
        qh = [A.take([NT], BF16) for _ in range(2)]
        kh = [A.take([NT], BF16) for _ in range(2)]
        ve = [A.take([18, 128], BF16) for _ in range(2)]
        vo = [A.take([15, 128], BF16) for _ in range(2)]
        Bqh = [Buf(f"qh{i}") for i in range(2)]
        Bkh = [Buf(f"kh{i}") for i in range(2)]
        Bve = [Buf(f"ve{i}") for i in range(2)]
        Bvo = [Buf(f"vo{i}") for i in range(2)]
        enx = self.stager(5, 512)
        rsn = self.stager(2, 256, F32)
        scale = 128 ** -0.5
        for h in range(16):
            b = h % 2
            S.dma("sp", lambda e, h=h, b=b: e.dma_start(out=qh[b], in_=self.qT[h, :, :]), Bqh[b],
                  reads=[Bq[(h, t)] for t in range(5)], writes=[Bqh[b]])
            S.dma("sp", lambda e, h=h, b=b: e.dma_start(out=kh[b], in_=self.kT[h, :, :]), Bkh[b],
                  reads=[Bk[(h, t)] for t in range(5)], writes=[Bkh[b]])
            S.dma("sp", lambda e, h=h, b=b: e.dma_start(
                out=ve[b], in_=self.vd[:, h * 128:(h + 1) * 128].rearrange("(t p) d -> p t d", p=128)), Bve[b],
                reads=Bv, writes=[Bve[b]])
            S.dma("sp", lambda e, h=h, b=b: e.dma_start(
                out=vo[b], in_=self.vd[64:64 + 15 * 128, h * 128:(h + 1) * 128].rearrange("(t p) d -> p t d", p=128)), Bvo[b],
                reads=Bv, writes=[Bvo[b]])
            def row_S(r, h=h, b=b):
                sr = min(max(r - 4, 0), 24)
                ro0 = sr - r + 7
                q0 = 64 * r
                pst, Bps = self.ps(0, 4)
                ktiles = [(64 * sr + 128 * jt) for jt in range(4)] + [NL, NL + 128]
                for jt, k0 in enumerate(ktiles):
                    S.op("pe", lambda e, jt=jt, k0=k0: e.matmul(
                        pst[:, jt * 64:(jt + 1) * 64], lhsT=kh[b][:, k0:k0 + 128], rhs=qh[b][:, q0:q0 + 64],
                        start=True, stop=True), reads=[Bkh[b], Bqh[b]], writes=[Bps])
                ee, Be = enx()
                e3 = ee[:, 0:384].rearrange("p (a b) -> p a b", b=64)
                S.op("act", lambda e: e.activation(out=ee[:, 0:384], in_=pst[:, 0:384], func=AF.Exp, scale=scale),
                     reads=[Bps], writes=[Be])
                S.op("dve", lambda e: e.tensor_tensor(
                    out=e3[:, 0:4, :], in0=e3[:, 0:4, :], in1=tab[:, h, ro0:ro0 + 7:2, :], op=ALU.mult),
                    reads=[Be, Btab], writes=[Be])
                return (r, sr, q0, e3, Be)

            def row_PV(ctx, h=h, b=b):
                r, sr, q0, e3, Be = ctx
                o_ps, Bo = self.ps(4, 6)
                s_ps, Bs = self.ps(6, 8)
                for jt in range(6):
                    if jt < 4:
                        if sr % 2 == 0:
                            vt, Bvt = ve[b][:, sr // 2 + jt, :], Bve[b]
                        else:
                            vt, Bvt = vo[b][:, (sr - 1) // 2 + jt, :], Bvo[b]
                    else:
                        vt, Bvt = ve[b][:, 16 + (jt - 4), :], Bve[b]
                    S.op("pe", lambda e, vt=vt, jt=jt: e.matmul(
                        o_ps[:, 0:64], lhsT=vt, rhs=e3[:, jt, :], start=(jt == 0), stop=(jt == 5)),
                        reads=[Bvt, Be], writes=[Bo])
                for jt in range(6):
                    S.op("pe", lambda e, jt=jt: e.matmul(
                        s_ps[:, 0:64], lhsT=self.ones[:, :], rhs=e3[:, jt, :], start=(jt == 0), stop=(jt == 5)),
                        reads=[self.Bones, Be], writes=[Bs])
                self.attn_finalize(o_ps, Bo, s_ps, Bs, 64, self.hT[:, h, q0:q0 + 64], self.Bh[(h, self.tt_of(q0))], rsn)

            def ctx_S(h=h, b=b):
                pst, Bps = self.ps(0, 4)
                for jt in range(2):
                    S.op("pe", lambda e, jt=jt: e.matmul(
                        pst[:, jt * 256:(jt + 1) * 256], lhsT=kh[b][:, NL + jt * 128:NL + (jt + 1) * 128], rhs=qh[b][:, NL:NT],
                        start=True, stop=True), reads=[Bkh[b], Bqh[b]], writes=[Bps])
                ee, Be = enx()
                S.op("act", lambda e: e.activation(out=ee[:, 0:512], in_=pst[:, 0:512], func=AF.Exp, scale=scale),
                     reads=[Bps], writes=[Be])
                return (ee, Be)

            def ctx_PV(ctx, h=h, b=b):
                ee, Be = ctx
                o_ps, Bo = self.ps(4, 6)
                s_ps, Bs = self.ps(6, 8)
                for jt in range(2):
                    S.op("pe", lambda e, jt=jt: e.matmul(
                        o_ps[:, 0:256], lhsT=ve[b][:, 16 + jt, :], rhs=ee[:, jt * 256:(jt + 1) * 256], start=(jt == 0), stop=(jt == 1)),
                        reads=[Bve[b], Be], writes=[Bo])
                for jt in range(2):
                    S.op("pe", lambda e, jt=jt: e.matmul(
                        s_ps[:, 0:256], lhsT=self.ones[:, :], rhs=ee[:, jt * 256:(jt + 1) * 256], start=(jt == 0), stop=(jt == 1)),
                        reads=[self.Bones, Be], writes=[Bs])
                self.attn_finalize(o_ps, Bo, s_ps, Bs, 256, self.hT[:, h, NL:NT], self.Bh[(h, 4)], rsn)

            pend = []
            for r in range(32):
                pend.append(row_S(r))
                if len(pend) > 2:
                    row_PV(pend.pop(0))
            cctx = ctx_S()
            while pend:
                row_PV(pend.pop(0))
            ctx_PV(cctx)
        S.barrier()
        A.reset()
        self.psi = 0
        self.oproj(self.I["a_w_o"], j * 2048)

    def rms_rstd(self, pst, Bps, w, n, dst, Bdst):
        S = self.S
        S.op("act", lambda e: e.activation(out=dst[:, 0:w], in_=pst[:, 0:w], func=AF.Ln, bias=self.epsb[:, 0:1], scale=1.0 / n),
             reads=[Bps, self.Beps], writes=[Bdst])
        S.op("act", lambda e: e.activation(out=dst[:, 0:w], in_=dst[:, 0:w], func=AF.Exp, scale=-0.5), reads=[Bdst], writes=[Bdst])

    def phase_mla(self, layer):
        S, A = self.S, self.A
        hB = lambda k, t: self.Bh[(k, t)]
        Bq = {(h, t): Buf(f"mqd{h}_{t}") for h in range(16) for t in range(5)}
        Bk = {(h, t): Buf(f"mkd{h}_{t}") for h in range(16) for t in range(5)}
        Bqp = {(h, t): Buf(f"mqpd{h}_{t}") for h in range(16) for t in range(5)}
        Bv = [Buf(f"mvd{n}") for n in range(18)]
        cqn = A.take([4, NT], BF16)
        ckvn = A.take([4, NT], BF16)
        kpe = A.take([NT], BF16)
        rope = A.take([NT], F32)
        fold = A.take([128], BF16)
        Bfold = Buf("fold")
        S.dma("pool", lambda e: e.dma_start(out=fold, in_=self.I["fold"]), Bfold, writes=[Bfold])
        prod = [A.take([512], BF16) for _ in range(2)]
        Bprod = [Buf(f"prod{i}") for i in range(2)]
        pri = {"i": 0}

        def rope_fold(pz, Bpz, t0, w):
            i = pri["i"] % 2
            pri["i"] += 1
            S.op("dve", lambda e: e.tensor_tensor(out=prod[i][:, 0:w], in0=pz[:, 0:w], in1=rope[:, t0:t0 + w], op=ALU.mult),
                 reads=[Bpz, Brope], writes=[Bprod[i]])
            pf, Bpf = self.ps()
            S.op("pe", lambda e: e.matmul(pf[:, 0:w], lhsT=fold, rhs=prod[i][:, 0:w], start=True, stop=True),
                 reads=[Bfold, Bprod[i]], writes=[Bpf])
            return pf, Bpf
        Bcqn = {(c, t): Buf(f"cqn{c}_{t}") for c in range(4) for t in range(5)}
        Bckvn = {(c, t): Buf(f"ckvn{c}_{t}") for c in range(4) for t in range(5)}
        Bkpe = Buf("kpe")
        Brope = Buf("rope")
        S.dma("sp", lambda e: e.dma_start(out=rope, in_=self.I["rope"]), Brope, writes=[Brope])
        mark = A.mark()
        if os.environ.get("MLA_STOP") == "0":
            self.dbg_sb(self.hT[:, 0, 0:2048], None, 0, 128)
            self.dbg_sb(self.hT[:, 7, 0:2048], None, 128, 128)
            return
        wres = A.take([16, 1152], BF16)
        Bwres = [Buf(f"wres{i}") for i in range(3)]
        for i, (c0, cw) in enumerate(((0, 512), (512, 512), (1024, 128))):
            S.dma("pool", lambda e, c0=c0, cw=cw: e.dma_start(
                out=wres[:, :, c0:c0 + cw], in_=self.I["b_w_in"][:, c0:c0 + cw].rearrange("(k p) n -> p k n", p=128)),
                Bwres[i], writes=[Bwres[i]])
        craw = A.take([4, 512], F32)
        Bcraw = Buf("craw")
        sqb = A.take([4, 512], BF16)
        Bsqb = Buf("sqb")
        rq = A.take([512], F32)
        Brq = Buf("rq")
        t1 = A.take([512], F32)
        t2 = A.take([512], F32)
        Bt1, Bt2 = Buf("t1"), Buf("t2")
        oqn, _ = CONST_LAYOUT["q_norm"]
        okn, _ = CONST_LAYOUT["kv_norm"]
        for t, (t0, w) in enumerate(TT):
            for half, (dstn, Bdstn, on) in enumerate(((cqn, Bcqn, oqn), (ckvn, Bckvn, okn))):
                for c in range(4):
                    col = half * 512 + c * 128
                    pst, Bps = self.ps()
                    for k in range(16):
                        S.op("pe", lambda e, k=k, col=col, t0=t0, w=w, pst=pst: e.matmul(
                            pst[:, 0:w], lhsT=wres[:, k, col:col + 128], rhs=self.hT[:, k, t0:t0 + w],
                            start=(k == 0), stop=(k == 15)), reads=[Bwres[half], self.Bh[(k, t)]], writes=[Bps])
                    S.op("dve", lambda e, c=c, w=w, pst=pst: e.tensor_copy(out=craw[:, c, 0:w], in_=pst[:, 0:w]),
                         reads=[Bps], writes=[Bcraw])
                S.op("act", lambda e, w=w: e.activation(out=sqb[:, :, 0:w], in_=craw[:, :, 0:w], func=AF.Square),
                     reads=[Bcraw], writes=[Bsqb])
                pst, Bps = self.ps()
                for c in range(4):
                    S.op("pe", lambda e, c=c, w=w, pst=pst: e.matmul(pst[:, 0:w], lhsT=self.ones[:, :], rhs=sqb[:, c, 0:w],
                                                                      start=(c == 0), stop=(c == 3)),
                         reads=[Bsqb, self.Bones], writes=[Bps])
                self.rms_rstd(pst, Bps, w, 512, rq, Brq)
                for c in range(4):
                    gcol = self.cst[:, on + c:on + c + 1]
                    S.op("dve", lambda e, c=c, w=w, t0=t0, dstn=dstn, gcol=gcol: e.scalar_tensor_tensor(
                        out=dstn[:, c, t0:t0 + w], in0=craw[:, c, 0:w], scalar=gcol, in1=rq[:, 0:w], op0=ALU.mult, op1=ALU.mult),
                        reads=[Bcraw, Brq, self.Bcst], writes=[Bdstn[(c, t)]])
            pz, Bpz = self.ps()
            for k in range(16):
                S.op("pe", lambda e, k=k, t0=t0, w=w, pz=pz: e.matmul(
                    pz[:, 0:w], lhsT=wres[:, k, 1024:1152], rhs=self.hT[:, k, t0:t0 + w],
                    start=(k == 0), stop=(k == 15)), reads=[Bwres[2], self.Bh[(k, t)]], writes=[Bpz])
            pf, Bpf = rope_fold(pz, Bpz, t0, w)
            S.op("act", lambda e, t0=t0, w=w, pf=pf: e.copy(out=kpe[:, t0:t0 + w], in_=pf[:, 0:w]), reads=[Bpf], writes=[Bkpe])
        S.barrier()
        A.reset(mark)
        self.psi = 0
        self.wstream_init(nbuf=3, kc=4, cb=512)
        nxt = self.stager(3, 512)
        t1 = A.take([512], F32)
        t2 = A.take([512], F32)
        Bt1, Bt2 = Buf("t1b"), Buf("t2b")
        cqB = lambda c, t: Bcqn[(c, t)]
        ckB = lambda c, t: Bckvn[(c, t)]
        for blk in range(4):
            wb, Bw = self.wload(self.I["b_w_uq_n"], 0, 4, blk * 512, 512)
            for oc in range(4):
                h = blk * 4 + oc
                self.proj_fm(wb, Bw, oc, 4, cqn, cqB, lambda pst, Bps, t, t0, w, h=h: self.evac_store(
                    nxt, pst[:, 0:w], Bps, self.qT[h, :, t0:t0 + w], [Bq[(h, t)]]))
        for blk in range(4):
            wb, Bw = self.wload(self.I["b_w_uk"], 0, 4, blk * 512, 512)
            for oc in range(4):
                h = blk * 4 + oc
                self.proj_fm(wb, Bw, oc, 4, ckvn, ckB, lambda pst, Bps, t, t0, w, h=h: self.evac_store(
                    nxt, pst[:, 0:w], Bps, self.kT[h, :, t0:t0 + w], [Bk[(h, t)]]))
        for blk in range(4):
            wb, Bw = self.wload(self.I["b_w_uv"], 0, 4, blk * 512, 512)
            self.proj_tm(wb, Bw, 512, 4, ckvn, ckB, lambda pst, Bps, n, blk=blk: self.evac_store(
                nxt, pst[:, 0:512], Bps, self.vd[n * 128:(n + 1) * 128, blk * 512:(blk + 1) * 512], [Bv[n]]))
        for blk in range(4):
            wb, Bw = self.wload(self.I["b_w_uq_r"], 0, 4, blk * 512, 512)
            for oc in range(4):
                h = blk * 4 + oc
                for t, (t0, w) in enumerate(TT):
                    pz, Bpz = self.ps()
                    for k in range(4):
                        S.op("pe", lambda e, k=k, oc=oc, t0=t0, w=w, pz=pz, wb=wb: e.matmul(
                            pz[:, 0:w], lhsT=wb[:, k, oc * 128:(oc + 1) * 128], rhs=cqn[:, k, t0:t0 + w],
                            start=(k == 0), stop=(k == 3)), reads=[Bw, Bcqn[(k, t)]], writes=[Bpz])
                    pf, Bpf = rope_fold(pz, Bpz, t0, w)
                    self.evac_store(nxt, pf[:, 0:w], Bpf, self.qpT[h, :, t0:t0 + w], [Bqp[(h, t)]])
        S.barrier()
        A.reset(mark)
        self.psi = 0
        if os.environ.get("MLA_STOP") == "2":
            self.dbg_rows(self.qT[0, :, 0:2048], 0, 128, BF16)
            self.dbg_rows(self.kT[0, :, 0:2048], 128, 128, BF16)
            self.dbg_rows(self.qpT[0, 0:64, 0:2048], 256, 64, BF16)
            self.dbg_rows(self.vd[0:128, :], 384, 128, BF16)
            self.dbg_sb(kpe[0:64, 0:2048], Bkpe, 512, 64)
            self.dbg_sb(cqn[:, 0, 0:2048], None, 640, 128)
            return
        qh = [A.take([NT], BF16) for _ in range(2)]
        qph = [A.take([NT], BF16) for _ in range(2)]
        kh = [A.take([NT], BF16) for _ in range(2)]
        ve = [A.take([18, 128], BF16) for _ in range(2)]
        Bqh = [Buf(f"mqh{i}") for i in range(2)]
        Bqph = [Buf(f"mqph{i}") for i in range(2)]
        Bkh = [Buf(f"mkh{i}") for i in range(2)]
        Bve = [Buf(f"mve{i}") for i in range(2)]
        enx = self.stager(4, 512)
        rsn = self.stager(2, 512, F32)
        scale = 192 ** -0.5
        for h in range(16):
            b = h % 2
            S.dma("sp", lambda e, h=h, b=b: e.dma_start(out=qh[b], in_=self.qT[h, :, :]), Bqh[b],
                  reads=[Bq[(h, t)] for t in range(5)], writes=[Bqh[b]])
            S.dma("sp", lambda e, h=h, b=b: e.dma_start(out=qph[b], in_=self.qpT[h, :, :]), Bqph[b],
                  reads=[Bqp[(h, t)] for t in range(5)], writes=[Bqph[b]])
            S.dma("sp", lambda e, h=h, b=b: e.dma_start(out=kh[b], in_=self.kT[h, :, :]), Bkh[b],
                  reads=[Bk[(h, t)] for t in range(5)], writes=[Bkh[b]])
            S.dma("sp", lambda e, h=h, b=b: e.dma_start(
                out=ve[b], in_=self.vd[:, h * 128:(h + 1) * 128].rearrange("(t p) d -> p t d", p=128)), Bve[b],
                reads=Bv, writes=[Bve[b]])
            for t, (t0, w) in enumerate(TT):
                kts = list(range(18)) if t0 < NL else [16, 17]
                o_ps, Bo = self.ps(4, 6)
                s_ps, Bs = self.ps(6, 8)
                n = len(kts)

                def pv(i, kt, ee, Be, b=b, w=w, o_ps=o_ps, s_ps=s_ps, Bo=Bo, Bs=Bs, n=n):
                    S.op("pe", lambda e: e.matmul(o_ps[:, 0:w], lhsT=ve[b][:, kt, :], rhs=ee[:, 0:w], start=(i == 0), stop=(i == n - 1)),
                         reads=[Bve[b], Be], writes=[Bo])
                    S.op("pe", lambda e: e.matmul(s_ps[:, 0:w], lhsT=self.ones[:, :], rhs=ee[:, 0:w], start=(i == 0), stop=(i == n - 1)),
                         reads=[self.Bones, Be], writes=[Bs])

                pend = []
                for i, kt in enumerate(kts):
                    pst, Bps = self.ps(0, 4)
                    S.op("pe", lambda e, b=b, kt=kt, t0=t0, w=w, pst=pst: e.matmul(
                        pst[:, 0:w], lhsT=kh[b][:, kt * 128:(kt + 1) * 128], rhs=qh[b][:, t0:t0 + w], start=True, stop=False),
                        reads=[Bkh[b], Bqh[b]], writes=[Bps])
                    S.op("pe", lambda e, b=b, kt=kt, t0=t0, w=w, pst=pst: e.matmul(
                        pst[:, 0:w], lhsT=kpe[:, kt * 128:(kt + 1) * 128], rhs=qph[b][:, t0:t0 + w], start=False, stop=True),
                        reads=[Bkpe, Bqph[b]], writes=[Bps])
                    ee, Be = enx()
                    S.op("act", lambda e, ee=ee, pst=pst, w=w: e.activation(out=ee[:, 0:w], in_=pst[:, 0:w], func=AF.Exp, scale=scale),
                         reads=[Bps], writes=[Be])
                    pend.append((i, kt, ee, Be))
                    if len(pend) > 2:
                        pv(*pend.pop(0))
                while pend:
                    pv(*pend.pop(0))
                self.attn_finalize(o_ps, Bo, s_ps, Bs, w, self.hT[:, h, t0:t0 + w], self.Bh[(h, t)], rsn)
        S.barrier()
        A.reset()
        self.psi = 0
        self.oproj(self.I["b_w_o"], 0)

    def phase_sgu(self, layer):
        S, A = self.S, self.A
        hB = lambda k, t: self.Bh[(k, t)]
        Bu = {(c, t): Buf(f"ud{c}_{t}") for c in range(16) for t in range(5)}
        Bv = [Buf(f"svd{n}") for n in range(18)]
        w2d = self.I["c_w_in"]
        obu, _ = CONST_LAYOUT["b_u"]
        self.wstream_init()
        nxt = self.stager(3, 512)
        for blk in range(4):
            wb, Bw = self.wload(w2d, 0, 16, blk * 512, 512)
            for oc in range(4):
                c = blk * 4 + oc
                bcol = self.cst[:, obu + c:obu + c + 1]
                self.proj_fm(wb, Bw, oc, 16, self.hT, hB, lambda pst, Bps, t, t0, w, c=c, bcol=bcol: self.evac_store(
                    nxt, pst[:, 0:w], Bps, self.uT[c * 128:(c + 1) * 128, t0:t0 + w], [Bu[(c, t)]],
                    func=AF.Gelu, bias=bcol, Bbias=self.Bcst))
        S.barrier()
        A.reset()
        self.psi = 0
        if os.environ.get("SGU_STOP") == "1":
            return
        wv = A.take([16, 2048], BF16)
        Bwv = [Buf(f"wv{i}") for i in range(4)]
        for i in range(4):
            S.dma("pool", lambda e, i=i: e.dma_start(
                out=wv[:, :, i * 512:(i + 1) * 512], in_=w2d[:, 2048 + i * 512:2048 + (i + 1) * 512].rearrange("(k p) n -> p k n", p=128)),
                Bwv[i], writes=[Bwv[i]])
        rows = A.take([6144], F32)
        Brows = Buf("rows")
        S.dma("sp", lambda e: e.dma_start(out=rows, in_=self.I["rows"][:, 0:6144].broadcast_to([128, 6144])), Brows, writes=[Brows])
        vraw = [A.take([2048], F32) for _ in range(2)]
        Bvraw = [Buf(f"vraw{i}") for i in range(2)]
        sqt = A.take([2048], F32)
        Bsqt = Buf("sqt")
        vob = [A.take([2048], BF16) for _ in range(2)]
        Bvob = [Buf(f"vob{i}") for i in range(2)]
        st = A.take([8], F32)
        Bst = Buf("st")
        for n in range(18):
            b = n % 2
            t = self.tt_of(n * 128)
            for blk in range(4):
                pst, Bps = self.ps()
                for k in range(16):
                    S.op("pe", lambda e, k=k, n=n, blk=blk, pst=pst: e.matmul(
                        pst[:, 0:512], lhsT=self.hT[:, k, n * 128:(n + 1) * 128], rhs=wv[:, k, blk * 512:(blk + 1) * 512],
                        start=(k == 0), stop=(k == 15)), reads=[Bwv[blk], self.Bh[(k, t)]], writes=[Bps])
                S.op("dve", lambda e, b=b, blk=blk, pst=pst: e.tensor_tensor(
                    out=vraw[b][:, blk * 512:(blk + 1) * 512], in0=pst[:, 0:512], in1=rows[:, blk * 512:(blk + 1) * 512], op=ALU.add),
                    reads=[Bps, Brows], writes=[Bvraw[b]])
            S.op("act", lambda e, b=b: e.activation(out=vraw[b], in_=vraw[b], func=AF.Gelu), reads=[Bvraw[b]], writes=[Bvraw[b]])
            S.op("dve", lambda e, b=b: e.reduce_sum(out=st[:, 0:1], in_=vraw[b], axis=AX.X), reads=[Bvraw[b]], writes=[Bst])
            S.op("act", lambda e, b=b: e.activation(out=sqt, in_=vraw[b], func=AF.Square), reads=[Bvraw[b]], writes=[Bsqt])
            S.op("dve", lambda e: e.reduce_sum(out=st[:, 1:2], in_=sqt, axis=AX.X), reads=[Bsqt], writes=[Bst])
            S.op("dve", lambda e: e.tensor_scalar_mul(out=st[:, 2:3], in0=st[:, 0:1], scalar1=1.0 / 2048), reads=[Bst], writes=[Bst])
            S.op("dve", lambda e: e.tensor_tensor(out=st[:, 3:4], in0=st[:, 2:3], in1=st[:, 2:3], op=ALU.mult), reads=[Bst], writes=[Bst])
            S.op("dve", lambda e: e.scalar_tensor_tensor(out=st[:, 4:5], in0=st[:, 1:2], scalar=1.0 / 2048, in1=st[:, 3:4],
                                                         op0=ALU.mult, op1=ALU.subtract), reads=[Bst], writes=[Bst])
            S.op("act", lambda e: e.activation(out=st[:, 5:6], in_=st[:, 4:5], func=AF.Ln, bias=self.epsb[:, 0:1], scale=1.0),
                 reads=[Bst, self.Beps], writes=[Bst])
            S.op("act", lambda e: e.activation(out=st[:, 5:6], in_=st[:, 5:6], func=AF.Exp, scale=-0.5), reads=[Bst], writes=[Bst])
            S.op("dve", lambda e, b=b: e.tensor_scalar(out=vraw[b], in0=vraw[b], scalar1=st[:, 2:3], scalar2=st[:, 5:6],
                                                       op0=ALU.subtract, op1=ALU.mult), reads=[Bvraw[b], Bst], writes=[Bvraw[b]])
            S.op("dve", lambda e, b=b: e.tensor_tensor(out=vraw[b], in0=vraw[b], in1=rows[:, 2048:4096], op=ALU.mult),
                 reads=[Bvraw[b], Brows], writes=[Bvraw[b]])
            S.op("dve", lambda e, b=b: e.tensor_tensor(out=vob[b], in0=vraw[b], in1=rows[:, 4096:6144], op=ALU.add),
                 reads=[Bvraw[b], Brows], writes=[Bvob[b]])
            S.dma("sp", lambda e, n=n, b=b: e.dma_start(out=self.vd[n * 128:(n + 1) * 128, :], in_=vob[b]), Bvob[b],
                  reads=[Bvob[b]], writes=[Bv[n]])
        S.barrier()
        A.reset()
        self.psi = 0
        if os.environ.get("SGU_STOP") == "2":
            return
        wsT = A.take([16, 128], BF16)
        BwsT = Buf("wsT")
        S.dma("pool", lambda e: e.dma_start(out=wsT, in_=self.I["c_wsT"].rearrange("k (g p) -> k g p", p=128)), BwsT, writes=[BwsT])
        bsb = A.take([2048], F32)
        Bbsb = Buf("bsb")
        S.dma("sp", lambda e: e.dma_start(out=bsb, in_=self.I["rows"][:, 6144:8192].broadcast_to([128, 2048])), Bbsb, writes=[Bbsb])
        vn = [A.take([2048], BF16) for _ in range(2)]
        Bvn = [Buf(f"vn{i}") for i in range(2)]
        ut = [A.take([16, 128], BF16) for _ in range(2)]
        But = [Buf(f"ut{i}") for i in range(2)]
        tmx = [A.take([512], F32) for _ in range(2)]
        Btmx = [Buf(f"tmx{i}") for i in range(2)]
        ti = 0
        for n in range(18):
            b = n % 2
            t = self.tt_of(n * 128)
            S.dma("sp", lambda e, n=n, b=b: e.dma_start(out=vn[b], in_=self.vd[n * 128:(n + 1) * 128, :]), Bvn[b],
                  reads=[Bv[n]], writes=[Bvn[b]])
            S.dma("sp", lambda e, n=n, b=b: e.dma_start(
                out=ut[b], in_=self.uT[:, n * 128:(n + 1) * 128].rearrange("(g p) t -> p g t", p=128)), But[b],
                reads=[Bu[(c, t)] for c in range(16)], writes=[But[b]])
            for G in range(4):
                pst, Bps = self.ps()
                for gg in range(4):
                    g = 4 * G + gg
                    S.op("pe", lambda e, b=b, g=g, gg=gg, pst=pst: e.matmul(
                        pst[:, gg * 128:(gg + 1) * 128], lhsT=vn[b][:, g * 128:(g + 1) * 128], rhs=wsT[:, g, :], start=True, stop=True),
                        reads=[Bvn[b], BwsT], writes=[Bps])
                tb = ti % 2
                ti += 1
                S.op("dve", lambda e, G=G, tb=tb, pst=pst: e.tensor_tensor(
                    out=tmx[tb], in0=pst[:, 0:512], in1=bsb[:, G * 512:(G + 1) * 512], op=ALU.add),
                    reads=[Bps, Bbsb], writes=[Btmx[tb]])
                S.op("dve", lambda e, G=G, tb=tb, b=b, n=n: e.tensor_tensor(
                    out=self.hT[:, 4 * G:4 * G + 4, n * 128:(n + 1) * 128], in0=tmx[tb].rearrange("p (a b) -> p a b", b=128),
                    in1=ut[b][:, 4 * G:4 * G + 4, :], op=ALU.mult),
                    reads=[Btmx[tb], But[b]], writes=[self.Bh[(4 * G + gg, t)] for gg in range(4)])
        S.barrier()
        A.reset()
        self.psi = 0
        if os.environ.get("SGU_STOP") == "3":
            return
        self.oproj(self.I["c_w_o"], 0)


def _eps_setup(B):
    nc, S = B.nc, B.S
    B.epsb = nc.alloc_sbuf_tensor("epsb", [128, 1], F32)
    B.Beps = Buf("eps")
    S.op("dve", lambda e: e.memset(B.epsb[:, :], EPS), writes=[B.Beps])


_orig_build = Builder.build


def _build(self):
    _eps_setup(self)
    _orig_build(self)


Builder.build = _build


def full_plan():
    plan = [("ada",)]
    for i in range(DEPTH):
        kind, j = i % 3, i // 3
        plan.append(("norm", i, 0))
        plan.append((("na", i, j), ("mla", i), ("sgu", i))[kind])
        plan.append(("ffn", i))
    plan.append(("final",))
    return plan


_CACHE = {}


def kernel(**inputs):
    plan = full_plan()
    key = "full"
    if key not in _CACHE:
        _CACHE[key] = Builder(plan).nc
    nc = _CACHE[key]
    active = [0, 1, 4, 5]
    per_b = [host_layout(inputs, b) for b in range(4)]
    zero = {k: np.zeros_like(v) for k, v in per_b[0].items()}
    in_maps = [zero] * 8
    in_maps = list(in_maps)
    for b, c in enumerate(active):
        in_maps[c] = per_b[b]
    res = run_bass_kernel_spmd(nc, in_maps, core_ids=list(range(8)))
    out = np.stack([np.ascontiguousarray(np.asarray(res.results[c]["out"], np.float32).T) for c in active], axis=0)
    return out
```

```python
import os
import numpy as np
import concourse.bass as bass
import concourse.mybir as mybir
from concourse.bass_utils import run_bass_kernel_spmd

F32 = mybir.dt.float32
BF16 = mybir.dt.bfloat16
AF = mybir.ActivationFunctionType
ALU = mybir.AluOpType
AX = mybir.AxisListType

D = 2048
NL = 2048
NCX = 256
NT = NL + NCX
DEPTH = 4
DFF = 5632
EPS = 1e-6
TT = [(0, 512), (512, 512), (1024, 512), (1536, 512), (2048, 256)]
SAME_ENGINE_SYNC = True


class Buf:
    __slots__ = ("name", "w", "r", "dsem", "dcnt")

    def __init__(self, name):
        self.name = name
        self.w = None
        self.r = []
        self.dsem = None
        self.dcnt = 0


class Sched:
    ENGS = ("pe", "act", "dve", "pool", "sp")

    def __init__(self, nc):
        self.nc = nc
        self.ops = {e: [] for e in self.ENGS}
        self.nsem = 0
        self.esem = {e: self._sem("eng_" + e) for e in self.ENGS}
        self.dbufs = []
        self.free_dsems = []
        self.pending = {e: [] for e in self.ENGS}
        self.rr = 0

    def _sem(self, name):
        self.nsem += 1
        return self.nc.alloc_semaphore(name=name)

    def _hazards(self, eng, reads, writes):
        evs = self.pending[eng]
        self.pending[eng] = []
        for b in reads:
            if b.w is not None:
                evs.append(b.w)
        for b in writes:
            if b.w is not None:
                evs.append(b.w)
            evs.extend(b.r)
        return evs

    def op(self, eng, fn, reads=(), writes=()):
        lst = self.ops[eng]
        ev = ("e", eng, len(lst))
        waits = self._hazards(eng, reads, writes)
        lst.append({"fn": fn, "waits": waits, "sig": False, "dma": None})
        for b in writes:
            b.w = ev
            b.r = []
        for b in reads:
            b.r = [x for x in b.r if not (x[0] == "e" and x[1] == eng)]
            b.r.append(ev)
        return ev

    def dma(self, eng, fn, sbuf, reads=(), writes=()):
        if sbuf.dsem is None:
            if self.free_dsems:
                sbuf.dsem = self.free_dsems.pop()
            else:
                sbuf.dsem = [self._sem("dsem%d" % self.nsem), 0]
            self.dbufs.append(sbuf)
        sbuf.dsem[1] += 16
        ev = ("d", sbuf.dsem[0], sbuf.dsem[1])
        waits = self._hazards(eng, reads, writes)
        self.ops[eng].append({"fn": fn, "waits": waits, "sig": False, "dma": sbuf.dsem[0]})
        for b in writes:
            b.w = ev
            b.r = []
        for b in reads:
            b.r.append(ev)
        return ev

    def barrier(self):
        evs = []
        for e in ("pe", "act", "dve", "pool"):
            lst = self.ops[e]
            for idx in range(len(lst) - 1, -1, -1):
                if lst[idx]["dma"] is None:
                    evs.append(("e", e, idx))
                    break
        for b in self.dbufs:
            evs.append(("d", b.dsem[0], b.dsem[1]))
            self.free_dsems.append(b.dsem)
            b.dsem = None
        self.dbufs = []
        for e in self.ENGS:
            self.pending[e] = self.pending[e] + evs

    def alt(self):
        self.rr ^= 1
        return "act" if self.rr else "dve"

    def emit(self, final_events=()):
        nc = self.nc

        def skip(w, e):
            return w[0] == "e" and w[1] == e and (e in ("pe", "sp") or not SAME_ENGINE_SYNC)

        for e in self.ENGS:
            for o in self.ops[e]:
                for w in o["waits"]:
                    if w[0] == "e" and not skip(w, e):
                        self.ops[w[1]][w[2]]["sig"] = True
        for w in final_events:
            if w[0] == "e":
                self.ops[w[1]][w[2]]["sig"] = True
        cnt = {}
        for e in self.ENGS:
            c = 0
            arr = []
            for o in self.ops[e]:
                if o["sig"]:
                    c += 1
                arr.append(c)
            cnt[e] = arr

        def resolve(w):
            if w[0] == "e":
                return self.esem[w[1]], cnt[w[1]][w[2]]
            return w[1], w[2]

        def run(e, engine):
            waited = {}
            for o in self.ops[e]:
                need = {}
                for w in o["waits"]:
                    if skip(w, e):
                        continue
                    s, v = resolve(w)
                    k = id(s)
                    if waited.get(k, 0) >= v:
                        continue
                    if k not in need or need[k][1] < v:
                        need[k] = (s, v)
                for k, (s, v) in need.items():
                    engine.wait_ge(s, v)
                    waited[k] = v
                ins = o["fn"](engine)
                if o["dma"] is not None:
                    ins.then_inc(o["dma"], 16)
                elif o["sig"]:
                    ins.then_inc(self.esem[e], 1)
            if e == "sp":
                for w in final_events:
                    s, v = resolve(w)
                    engine.wait_ge(s, v)

        with nc.Block() as block:
            @block.tensor
            def _(eng):
                run("pe", eng)

            @block.scalar
            def _(eng):
                run("act", eng)

            @block.vector
            def _(eng):
                run("dve", eng)

            @block.gpsimd
            def _(eng):
                run("pool", eng)

            @block.sync
            def _(eng):
                run("sp", eng)


class Arena:
    def __init__(self, nc, nbytes):
        self.n4 = nbytes // 4
        self.t = nc.alloc_sbuf_tensor("arena", [128, self.n4], F32)
        self.off = 0
        self.cnt = 0

    def reset(self, mark=0):
        self.off = mark

    def mark(self):
        return self.off

    def take(self, free_shape, dtype, npart=128):
        n = 1
        for s in free_shape:
            n *= s
        esz = 4 if dtype == F32 else 2
        nb = (n * esz + 31) // 32 * 32
        o4 = self.off
        self.off += nb // 4
        assert self.off <= self.n4, f"arena overflow {self.off * 4} > {self.n4 * 4}"
        v = self.t[0:npart, o4:o4 + nb // 4]
        if dtype != F32:
            v = v.bitcast(dtype)
        v = v[:, 0:n]
        if len(free_shape) == 2:
            v = v.rearrange("p (a b) -> p a b", b=free_shape[1])
        elif len(free_shape) == 3:
            v = v.rearrange("p (a b c) -> p a b c", b=free_shape[1], c=free_shape[2])
        return v


def _fm(vec, n):
    return np.ascontiguousarray(np.asarray(vec, np.float32).reshape(n, 128).T)


CONST_LAYOUT = {}


def _const_offsets():
    off = 0
    lay = {}
    for name, n in (("ada_b", 4 * 96), ("norm_g", 4 * 2 * 16), ("final_g", 16), ("conv_w", 4 * 88 * 3),
                    ("conv_b", 4 * 88), ("q_norm", 4), ("kv_norm", 4), ("b_u", 16)):
        lay[name] = (off, n)
        off += n
    return lay, off


CONST_LAYOUT, NCONST = _const_offsets()
ROW_LAYOUT = {"b_v": (0, 2048), "ln_g": (2048, 2048), "ln_b": (4096, 2048), "bs": (6144, 2048)}
NROW = 8192


def _rope_tables():
    half = 32
    freqs = (10000.0 ** (-np.arange(0, half, 2, dtype=np.float32) / half)).astype(np.float32)
    t = np.arange(NL)
    rows = (t // 64).astype(np.float32)
    cols = (t % 64).astype(np.float32)
    ang_r = rows[None, :] * freqs[:, None]
    ang_c = cols[None, :] * freqs[:, None]
    C = np.zeros((64, NL), np.float32)
    Sn = np.zeros((64, NL), np.float32)
    for base, ang in ((0, ang_r), (32, ang_c)):
        cs, sn = np.cos(ang).astype(np.float32), np.sin(ang).astype(np.float32)
        C[base:base + 16] = cs
        C[base + 16:base + 32] = cs
        Sn[base:base + 16] = -sn
        Sn[base + 16:base + 32] = sn
    T = np.zeros((128, NT), np.float32)
    T[0:64, 0:NL] = C
    T[0:64, NL:] = 1.0
    T[64:128, 0:NL] = Sn
    return T


_ROPE_PERM = np.concatenate([np.arange(16, 32), np.arange(0, 16), np.arange(48, 64), np.arange(32, 48)])


def _na_table(rpb):
    q = np.arange(64)
    k = np.arange(64)
    cs = np.clip(q - 8, 0, 48)
    inwin = (k[:, None] >= cs[None, :]) & (k[:, None] <= cs[None, :] + 15)
    coff = np.clip(k[:, None] - q[None, :] + 15, 0, 30)
    g = rpb[:, :, coff]
    g = np.where(inwin[None, None], g, np.float32(-30000.0)).astype(np.float32)
    tab = np.empty((2, 64, 16, 14, 64), np.float32)
    for a2 in range(2):
        tab[a2] = g[:, a2:a2 + 14].transpose(2, 0, 1, 3)
    return np.ascontiguousarray(tab.reshape(128, 16 * 14 * 64))


def host_layout(inp, b):
    f = lambda a: np.ascontiguousarray(np.asarray(a, np.float32))
    m = {}
    m["xT_in"] = np.ascontiguousarray(np.concatenate([np.asarray(inp["x"][b], np.float32), np.asarray(inp["ctx"][b], np.float32)], axis=0).T)
    cv = np.empty((128, 16, 2), np.float32)
    cv[:, :, 0] = _fm(inp["c"][b], 16)
    cv[:, :, 1] = _fm(inp["c_ctx"], 16)
    m["cvec"] = cv
    cst = np.zeros((128, NCONST), np.float32)

    def put(name, arr):
        o, n = CONST_LAYOUT[name]
        cst[:, o:o + n] = arr.reshape(128, n)

    put("ada_b", np.stack([_fm(inp["ada_b"][i], 96) for i in range(4)], axis=1))
    put("norm_g", np.stack([_fm(inp["norm_g"][i, j], 16) for i in range(4) for j in range(2)], axis=1))
    put("final_g", _fm(inp["final_g"], 16))
    cw = np.stack([np.stack([_fm(inp["f_conv_w"][i, t], 88) for t in range(3)], axis=2) for i in range(4)], axis=1)
    put("conv_w", cw)
    put("conv_b", np.stack([_fm(inp["f_conv_b"][i], 88) for i in range(4)], axis=1))
    put("q_norm", _fm(inp["b_q_norm"][0], 4))
    put("kv_norm", _fm(inp["b_kv_norm"][0], 4))
    put("b_u", _fm(inp["c_b_in"][0][:2048], 16))
    m["consts"] = cst
    rows = np.zeros((1, NROW), np.float32)
    rows[0, 0:2048] = inp["c_b_in"][0][2048:]
    rows[0, 2048:4096] = inp["c_ln_g"][0]
    rows[0, 4096:6144] = inp["c_ln_b"][0]
    rows[0, 6144:8192] = np.asarray(inp["c_bs"][0]).reshape(-1)
    m["rows"] = rows
    m["ident"] = np.eye(128, dtype=np.float32)
    m["rope"] = _rope_tables()
    fold = np.zeros((128, 128), np.float32)
    fold[np.arange(64), np.arange(64)] = 1.0
    fold[np.arange(64) + 64, np.arange(64)] = 1.0
    m["fold"] = fold
    m["ada_w"] = f(inp["ada_w"]).reshape(4 * 2048, 6 * D)
    m["a_w_qkv"] = f(inp["a_w_qkv"]).reshape(2 * 2048, 6144)
    m["a_w_o"] = f(inp["a_w_o"]).reshape(2 * 2048, 2048)
    m["na_tab"] = np.concatenate([_na_table(np.asarray(inp["a_rpb"][j], np.float32)) for j in range(2)], axis=0)
    w_in = np.asarray(inp["b_w_in"][0], np.float32)
    m["b_w_in"] = np.ascontiguousarray(np.concatenate([w_in, w_in[:, 1024:1088][:, _ROPE_PERM]], axis=1))
    w_uq = np.asarray(inp["b_w_uq"][0], np.float32).reshape(512, 16, 192)
    m["b_w_uq_n"] = np.ascontiguousarray(w_uq[:, :, :128].reshape(512, 2048))
    m["b_w_uq_r"] = np.ascontiguousarray(
        np.concatenate([w_uq[:, :, 128:], w_uq[:, :, 128:][:, :, _ROPE_PERM]], axis=2).reshape(512, 2048))
    w_ukv = np.asarray(inp["b_w_ukv"][0], np.float32).reshape(512, 16, 256)
    m["b_w_uk"] = np.ascontiguousarray(w_ukv[:, :, :128].reshape(512, 2048))
    m["b_w_uv"] = np.ascontiguousarray(w_ukv[:, :, 128:].reshape(512, 2048))
    m["b_w_o"] = f(inp["b_w_o"][0])
    m["c_w_in"] = f(inp["c_w_in"][0])
    m["c_wsT"] = np.ascontiguousarray(np.asarray(inp["c_ws"][0], np.float32).transpose(2, 0, 1).reshape(128, 2048))
    m["c_w_o"] = f(inp["c_w_o"][0])
    m["f_w_in"] = f(inp["f_w_in"]).reshape(4 * 2048, 2 * DFF)
    m["f_w_out"] = f(inp["f_w_out"]).reshape(4 * DFF, 2048)
    return m


INPUT_SHAPES = {
    "xT_in": [D, NT], "cvec": [128, 16, 2], "consts": [128, NCONST], "rows": [1, NROW], "ident": [128, 128],
    "rope": [128, NT], "fold": [128, 128], "ada_w": [4 * 2048, 6 * D], "a_w_qkv": [2 * 2048, 6144], "a_w_o": [2 * 2048, 2048],
    "na_tab": [2 * 128, 16 * 14 * 64], "b_w_in": [2048, 1152], "b_w_uq_n": [512, 2048], "b_w_uq_r": [512, 2048],
    "b_w_uk": [512, 2048], "b_w_uv": [512, 2048], "b_w_o": [2048, 2048], "c_w_in": [2048, 4096],
    "c_wsT": [128, 2048], "c_w_o": [2048, 2048], "f_w_in": [4 * 2048, 2 * DFF], "f_w_out": [4 * DFF, 2048],
}


class Builder:
    def __init__(self, plan, debug_out=None):
        self.plan = plan
        nc = self.nc = bass.Bass("TRN2", target_bir_lowering=False)
        self.S = Sched(nc)
        self.I = {k: nc.dram_tensor(k, shp, F32, kind="ExternalInput").ap() for k, shp in INPUT_SHAPES.items()}
        self.out = nc.dram_tensor("out", [D, NL], F32, kind="ExternalOutput").ap()
        self.xT = nc.dram_tensor("xT_s", [D, NT], F32).ap()
        self.x_src = self.I["xT_in"]
        self.qT = nc.dram_tensor("qT_s", [16, 128, NT], BF16).ap()
        self.kT = nc.dram_tensor("kT_s", [16, 128, NT], BF16).ap()
        self.qpT = nc.dram_tensor("qpT_s", [16, 128, NT], BF16).ap()
        self.vd = nc.dram_tensor("v_s", [NT, D], BF16).ap()
        self.uT = nc.dram_tensor("uT_s", [D, NT], BF16).ap()
        self.BxT = {(k, t): Buf(f"xT{k}_{t}") for k in range(16) for t in range(5)}
        self.Bq = [Buf(f"qd{h}") for h in range(16)]
        self.Bk = [Buf(f"kd{h}") for h in range(16)]
        self.Bqp = [Buf(f"qpd{h}") for h in range(16)]
        self.Bvd = Buf("vd")
        self.BuT = Buf("uTd")
        self.hT = nc.alloc_sbuf_tensor("hT", [128, 16, NT], BF16)
        self.Bh = {(k, t): Buf(f"h{k}_{t}") for k in range(16) for t in range(5)}
        self.cst = nc.alloc_sbuf_tensor("cst", [128, NCONST], F32)
        self.Bcst = Buf("cst")
        self.mod = nc.alloc_sbuf_tensor("mod", [128, 4, 96, 2], F32)
        self.Bmod = Buf("mod")
        self.ones = nc.alloc_sbuf_tensor("ones", [128, 128], BF16)
        self.Bones = Buf("ones")
        self.ident = nc.alloc_sbuf_tensor("identsb", [128, 128], F32)
        self.Bident = Buf("ident")
        self.ab = nc.alloc_sbuf_tensor("ab", [128, 3, 2, 16], F32)
        self.Bab = Buf("ab")
        self.pb = [nc.alloc_psum_tensor(f"pb{i}", [128, 512], F32) for i in range(8)]
        self.Bpb = [Buf(f"pb{i}") for i in range(8)]
        self.psi = 0
        rem = nc.sbuf_bytes_remaining
        self.A = Arena(nc, (rem - 2048) // 32 * 32)
        self.final_events = []
        self.debug_out = debug_out
        self.build()

    def cv(self, name):
        o, n = CONST_LAYOUT[name]
        return self.cst[:, o:o + n]

    def ps(self, lo=0, hi=8):
        i = lo + (self.psi % (hi - lo))
        self.psi += 1
        return self.pb[i], self.Bpb[i]

    def tt_of(self, t0):
        return min(t0 // 512, 4)

    def build(self):
        S = self.S
        I = self.I
        S.dma("sp", lambda e: e.dma_start(out=self.cst[:, :], in_=I["consts"]), self.Bcst, writes=[self.Bcst])
        S.dma("sp", lambda e: e.dma_start(out=self.ident[:, :], in_=I["ident"]), self.Bident, writes=[self.Bident])
        S.op("dve", lambda e: e.memset(self.ones[:, :], 1.0), writes=[self.Bones])
        for step in self.plan:
            kind = step[0]
            self.A.reset()
            self.psi = 0
            if kind == "ada":
                self.phase_ada()
            elif kind == "init":
                self.phase_init()
            elif kind == "norm":
                self.phase_modnorm(step[1], step[2])
            elif kind == "ffn":
                self.phase_ffn(step[1])
            elif kind == "na":
                self.phase_na(step[1], step[2])
            elif kind == "mla":
                self.phase_mla(step[1])
            elif kind == "sgu":
                self.phase_sgu(step[1])
            elif kind == "final":
                if not getattr(self, "dbg", False):
                    self.phase_final()
            elif kind == "dump_xT":
                self.phase_dump_xT()
            else:
                raise ValueError(kind)
            S.barrier()
        S.emit(final_events=self.final_events)

    def wstream_init(self, nbuf=3, kc=16, cb=512):
        self.wb = [self.A.take([kc, cb], BF16) for _ in range(nbuf)]
        self.Bwb = [Buf(f"wb{i}_{self.A.cnt}") for i in range(nbuf)]
        self.A.cnt += 1
        self.wbi = 0

    def wload(self, w2d, r0, kc, c0, cb):
        i = self.wbi % len(self.wb)
        self.wbi += 1
        wb, B = self.wb[i], self.Bwb[i]
        src = w2d[r0:r0 + kc * 128, c0:c0 + cb].rearrange("(k p) n -> p k n", p=128)
        self.S.dma("pool", lambda e: e.dma_start(out=wb[:, 0:kc, 0:cb], in_=src), B, writes=[B])
        return wb, B

    def phase_ada(self):
        S, A = self.S, self.A
        cvt = A.take([16, 2], F32)
        Bcv = Buf("cvec")
        sT = A.take([16, 2], BF16)
        BsT = Buf("sT")
        S.dma("sp", lambda e: e.dma_start(out=cvt, in_=self.I["cvec"]), Bcv, writes=[Bcv])
        S.op("act", lambda e: e.activation(out=sT, in_=cvt, func=AF.Silu), reads=[Bcv], writes=[BsT])
        self.wstream_init(nbuf=int(os.environ.get("ADA_NBUF", "2")))
        for i in range(int(os.environ.get('ADA_LAYERS', DEPTH))):
            pst, Bps = self.ps(0, 2)
            for blk in range(int(os.environ.get('ADA_BLKS', 24))):
                wb, Bw = self.wload(self.I["ada_w"], i * 2048, 16, blk * 512, 512)
                for oc in range(4):
                    j = blk * 4 + oc
                    for k in range(16):
                        S.op("pe", lambda e, wb=wb, oc=oc, k=k, j=j, pst=pst: e.matmul(
                            pst[:, 2 * j:2 * j + 2], lhsT=wb[:, k, oc * 128:(oc + 1) * 128], rhs=sT[:, k, :],
                            start=(k == 0), stop=(k == 15)), reads=[Bw, BsT], writes=[Bps])
            o, n = CONST_LAYOUT["ada_b"]
            bias = self.cst[:, o + i * 96:o + (i + 1) * 96].unsqueeze(2).broadcast_to([128, 96, 2])
            S.op("dve", lambda e, i=i, pst=pst, bias=bias: e.tensor_tensor(
                out=self.mod[:, i, :, :], in0=pst[:, 0:192].rearrange("p (j s) -> p j s", s=2), in1=bias, op=ALU.add),
                reads=[Bps, self.Bcst], writes=[self.Bmod])

    def set_ab(self, layer, sub):
        S = self.S
        o, n = CONST_LAYOUT["norm_g"]
        g = self.cst[:, o + (layer * 2 + sub) * 16:o + (layer * 2 + sub + 1) * 16].unsqueeze(1).broadcast_to([128, 2, 16])
        base = sub * 48
        shift = self.mod[:, layer, base:base + 16, :].rearrange("p c s -> p s c")
        scale = self.mod[:, layer, base + 16:base + 32, :].rearrange("p c s -> p s c")
        gate = self.mod[:, layer, base + 32:base + 48, :].rearrange("p c s -> p s c")
        S.op("dve", lambda e: e.scalar_tensor_tensor(out=self.ab[:, 0, :, :], in0=scale, scalar=1.0, in1=g,
                                                     op0=ALU.add, op1=ALU.mult),
             reads=[self.Bmod, self.Bcst], writes=[self.Bab])
        S.op("dve", lambda e: e.tensor_copy(out=self.ab[:, 1, :, :], in_=shift), reads=[self.Bmod], writes=[self.Bab])
        S.op("dve", lambda e: e.tensor_copy(out=self.ab[:, 2, :, :], in_=gate), reads=[self.Bmod], writes=[self.Bab])

    def phase_init(self):
        S, A = self.S, self.A
        xin = [A.take([D], F32) for _ in range(2)]
        Bxin = [Buf(f"xin{i}") for i in range(2)]
        xo = [A.take([16, 128], F32) for _ in range(2)]
        Bxo = [Buf(f"xo{i}") for i in range(2)]
        for n in range(NT // 128):
            b = n % 2
            S.dma("sp", lambda e, n=n, b=b: e.dma_start(out=xin[b], in_=self.I["x_tok"][n * 128:(n + 1) * 128, :]),
                  Bxin[b], writes=[Bxin[b]])
            for kb in range(4):
                pst, Bps = self.ps()
                for kk in range(4):
                    k = kb * 4 + kk
                    S.op("pe", lambda e, b=b, k=k, kk=kk, pst=pst: e.transpose(
                        out=pst[:, kk * 128:(kk + 1) * 128], in_=xin[b][:, k * 128:(k + 1) * 128], identity=self.ident[:, :]),
                        reads=[Bxin[b], self.Bident], writes=[Bps])
                eng = S.alt()
                dst = xo[b][:, kb * 4:(kb + 1) * 4, :]
                src = pst[:, :].rearrange("p (a b) -> p a b", b=128)
                if eng == "act":
                    S.op("act", lambda e, dst=dst, src=src: e.copy(out=dst, in_=src), reads=[Bps], writes=[Bxo[b]])
                else:
                    S.op("dve", lambda e, dst=dst, src=src: e.tensor_copy(out=dst, in_=src), reads=[Bps], writes=[Bxo[b]])
            t = self.tt_of(n * 128)
            S.dma("sp", lambda e, n=n, b=b: e.dma_start(
                out=self.xT[:, n * 128:(n + 1) * 128].rearrange("(k p) t -> p k t", p=128), in_=xo[b]),
                Bxo[b], reads=[Bxo[b]], writes=[self.BxT[(k, t)] for k in range(16)])

    def phase_modnorm(self, layer, sub, final=False):
        S, A = self.S, self.A
        if not final:
            self.set_ab(layer, sub)
        xsrc = self.x_src
        xt = [A.take([16, 512], F32) for _ in range(2)]
        Bxt = [Buf(f"mxt{i}") for i in range(2)]
        sq = [A.take([16, 512], BF16) for _ in range(2)]
        Bsq = [Buf(f"msq{i}") for i in range(2)]
        rs = [A.take([512], F32) for _ in range(2)]
        Brs = [Buf(f"mrs{i}") for i in range(2)]
        tmp = [A.take([512], F32) for _ in range(4)]
        Btmp = [Buf(f"mtmp{i}") for i in range(4)]
        ti = 0
        for t, (t0, w) in enumerate(TT):
            b = t % 2
            s = 1 if t0 >= NL else 0
            S.dma("sp", lambda e, b=b, t0=t0, w=w: e.dma_start(
                out=xt[b][:, :, 0:w], in_=xsrc[:, t0:t0 + w].rearrange("(k p) t -> p k t", p=128)),
                Bxt[b], reads=[self.BxT[(k, t)] for k in range(16)], writes=[Bxt[b]])
            S.op("act", lambda e, b=b, w=w: e.activation(out=sq[b][:, :, 0:w], in_=xt[b][:, :, 0:w], func=AF.Square),
                 reads=[Bxt[b]], writes=[Bsq[b]])
            pst, Bps = self.ps()
            for k in range(16):
                S.op("pe", lambda e, b=b, k=k, w=w, pst=pst: e.matmul(pst[:, 0:w], lhsT=self.ones[:, :], rhs=sq[b][:, k, 0:w],
                                                                      start=(k == 0), stop=(k == 15)),
                     reads=[Bsq[b], self.Bones], writes=[Bps])
            S.op("act", lambda e, b=b, w=w, pst=pst: e.activation(out=rs[b][:, 0:w], in_=pst[:, 0:w], func=AF.Ln,
                                                                  bias=self.epsb[:, 0:1], scale=1.0 / D),
                 reads=[Bps, self.Beps], writes=[Brs[b]])
            S.op("act", lambda e, b=b, w=w: e.activation(out=rs[b][:, 0:w], in_=rs[b][:, 0:w], func=AF.Exp, scale=-0.5),
                 reads=[Brs[b]], writes=[Brs[b]])
            for k in range(16):
                tb = ti % 4
                ti += 1
                Acol = self.ab[:, 0, s, k:k + 1]
                Bcol = self.ab[:, 1, s, k:k + 1]
                S.op("dve", lambda e, b=b, k=k, w=w, tb=tb, Acol=Acol: e.scalar_tensor_tensor(
                    out=tmp[tb][:, 0:w], in0=xt[b][:, k, 0:w], scalar=Acol, in1=rs[b][:, 0:w], op0=ALU.mult, op1=ALU.mult),
                    reads=[Bxt[b], Brs[b], self.Bab], writes=[Btmp[tb]])
                S.op("act", lambda e, k=k, w=w, t0=t0, tb=tb, Bcol=Bcol: e.activation(
                    out=self.hT[:, k, t0:t0 + w], in_=tmp[tb][:, 0:w], func=AF.Identity, bias=Bcol, scale=1.0),
                    reads=[Btmp[tb], self.Bab], writes=[self.Bh[(k, t)]])

    def resid_epilogue_factory(self, nbuf=6):
        S, A = self.S, self.A
        xr = [A.take([512], F32) for _ in range(nbuf)]
        Bxr = [Buf(f"xr{i}_{A.cnt}") for i in range(nbuf)]
        A.cnt += 1
        PD = nbuf - 3
        st = {"tiles": [], "next": 0}
        xsrc = self.x_src

        def plan(tiles):
            st["tiles"] = list(tiles)
            st["next"] = 0

        def issue(n):
            k, t, t0, w = st["tiles"][n]
            b = n % nbuf
            S.dma("sp", lambda e: e.dma_start(out=xr[b][:, 0:w], in_=xsrc[k * 128:(k + 1) * 128, t0:t0 + w]),
                  Bxr[b], reads=[self.BxT[(k, t)]], writes=[Bxr[b]])

        def pre(idx):
            while st["next"] < min(len(st["tiles"]), idx + PD + 1):
                issue(st["next"])
                st["next"] += 1

        def post(idx, pst, Bps):
            k, t, t0, w = st["tiles"][idx]
            b = idx % nbuf
            s_ = 1 if t0 >= NL else 0
            gcol = self.ab[:, 2, s_, k:k + 1]
            S.op("dve", lambda e: e.scalar_tensor_tensor(out=xr[b][:, 0:w], in0=pst[:, 0:w], scalar=gcol, in1=xr[b][:, 0:w],
                                                         op0=ALU.mult, op1=ALU.add),
                 reads=[Bps, Bxr[b], self.Bab], writes=[Bxr[b]])
            S.dma("sp", lambda e: e.dma_start(out=self.xT[k * 128:(k + 1) * 128, t0:t0 + w], in_=xr[b][:, 0:w]),
                  Bxr[b], reads=[Bxr[b]], writes=[self.BxT[(k, t)]])

        return plan, pre, post

    def oproj(self, w2d, r0):
        S = self.S
        self.wstream_init()
        plan, pre, post = self.resid_epilogue_factory()
        plan([(blk * 4 + oc, t, t0, w) for blk in range(4) for oc in range(4) for t, (t0, w) in enumerate(TT)])
        idx = 0
        for blk in range(4):
            wb, Bw = self.wload(w2d, r0, 16, blk * 512, 512)
            for oc in range(4):
                for t, (t0, w) in enumerate(TT):
                    pre(idx)
                    pst, Bps = self.ps()
                    for k in range(16):
                        S.op("pe", lambda e, wb=wb, oc=oc, k=k, t0=t0, w=w, pst=pst: e.matmul(
                            pst[:, 0:w], lhsT=wb[:, k, oc * 128:(oc + 1) * 128], rhs=self.hT[:, k, t0:t0 + w],
                            start=(k == 0), stop=(k == 15)), reads=[Bw, self.Bh[(k, t)]], writes=[Bps])
                    post(idx, pst, Bps)
                    idx += 1
        self.x_src = self.xT

    def phase_ffn(self, layer):
        S, A = self.S, self.A
        self.phase_modnorm(layer, 1)
        S.barrier()
        A.reset()
        self.psi = 0
        GROUPS = [9, 9, 9, 9, 8]
        JQM = 9
        act = A.take([JQM, NT], BF16)
        Bact = {(j, t): Buf(f"act{j}_{t}") for j in range(JQM) for t in range(5)}
        self.wstream_init(nbuf=4, kc=16, cb=256)
        win_wb, win_B = self.wb, self.Bwb
        self.wstream_init(nbuf=2, kc=JQM, cb=512)
        wout_wb, wout_B = self.wb, self.Bwb
        wsi = {"in": 0, "out": 0}

        def use(which):
            if which == "in":
                self.wb, self.Bwb, self.wbi = win_wb, win_B, wsi["in"]
            else:
                self.wb, self.Bwb, self.wbi = wout_wb, wout_B, wsi["out"]

        def done(which):
            wsi[which] = self.wbi
        plan, pre, post = self.resid_epilogue_factory(nbuf=6)
        ctiles = []
        t0 = 0
        while t0 < NL:
            w = min(410, NL - t0)
            ctiles.append((t0, w, t0 > 0, t0 + w < NL))
            t0 += w
        ctiles.append((NL, NCX, False, False))
        UW = 412
        ub = [[A.take([UW], F32) for _ in range(2)] for _ in range(2)]
        Bub = [[Buf(f"ub{x}{i}") for i in range(2)] for x in range(2)]
        cva = [A.take([UW], F32) for _ in range(2)]
        Bcva = [Buf(f"cva{i}") for i in range(2)]
        cvg = [A.take([UW], F32) for _ in range(2)]
        Bcvg = [Buf(f"cvg{i}") for i in range(2)]
        sg = [A.take([UW], F32) for _ in range(2)]
        Bsg = [Buf(f"sg{i}") for i in range(2)]
        ocw, _ = CONST_LAYOUT["conv_w"]
        ocb, _ = CONST_LAYOUT["conv_b"]
        wi = self.I["f_w_in"]
        wo = self.I["f_w_out"]
        it = 0
        gstart = 0
        for q, JQ in enumerate(GROUPS):
            for jb in range(0, JQ, 2):
                nj = min(2, JQ - jb)
                ja0 = gstart + jb
                use("in")
                wa, Bwa = self.wload(wi, layer * 2048, 16, ja0 * 128, nj * 128)
                wg, Bwg = self.wload(wi, layer * 2048, 16, DFF + ja0 * 128, nj * 128)
                done("in")
                for jj in range(nj):
                    jl = jb + jj
                    ja = ja0 + jj
                    cw = lambda c, tap: self.cst[:, ocw + ((layer * 88 + c) * 3 + tap):ocw + ((layer * 88 + c) * 3 + tap) + 1]
                    cb = lambda c: self.cst[:, ocb + layer * 88 + c:ocb + layer * 88 + c + 1]
                    for (c0, w, hl, hr) in ctiles:
                        r = it % 2
                        it += 1
                        m0 = c0 - (1 if hl else 0)
                        mw = w + (1 if hl else 0) + (1 if hr else 0)
                        uo = 0 if hl else 1
                        tlist = sorted(set(self.tt_of(x) for x in (m0, m0 + mw - 1)))
                        hreads = lambda k: [self.Bh[(k, t)] for t in tlist]
                        pa, Bpa = self.ps()
                        for k in range(16):
                            S.op("pe", lambda e, wa=wa, jj=jj, k=k, m0=m0, mw=mw, pa=pa: e.matmul(
                                pa[:, 0:mw], lhsT=wa[:, k, jj * 128:(jj + 1) * 128], rhs=self.hT[:, k, m0:m0 + mw],
                                start=(k == 0), stop=(k == 15)), reads=[Bwa] + hreads(k), writes=[Bpa])
                        pg, Bpg = self.ps()
                        for k in range(16):
                            S.op("pe", lambda e, wg=wg, jj=jj, k=k, m0=m0, mw=mw, pg=pg: e.matmul(
                                pg[:, 0:mw], lhsT=wg[:, k, jj * 128:(jj + 1) * 128], rhs=self.hT[:, k, m0:m0 + mw],
                                start=(k == 0), stop=(k == 15)), reads=[Bwg] + hreads(k), writes=[Bpg])
                        for x, (pp, Bpp) in enumerate(((pa, Bpa), (pg, Bpg))):
                            u, Bu = ub[x][r], Bub[x][r]
                            S.op("act", lambda e, u=u, pp=pp, uo=uo, mw=mw: e.copy(out=u[:, uo:uo + mw], in_=pp[:, 0:mw]),
                                 reads=[Bpp], writes=[Bu])
                            if not hl:
                                S.op("act", lambda e, u=u: e.memzero(u[:, 0:1]), writes=[Bu])
                            if not hr:
                                S.op("act", lambda e, u=u, w=w: e.memzero(u[:, w + 1:w + 2]), writes=[Bu])
                        for x, (cvx, Bcvx, eng) in enumerate(((cva[r], Bcva[r], "dve"), (cvg[r], Bcvg[r], "dve"))):
                            u, Bu = ub[x][r], Bub[x][r]
                            c = ja if x == 0 else 44 + ja
                            S.op(eng, lambda e, cvx=cvx, u=u, w=w, c=c: e.tensor_scalar_mul(
                                out=cvx[:, 0:w], in0=u[:, 0:w], scalar1=cw(c, 0)),
                                reads=[Bu, self.Bcst], writes=[Bcvx])
                            S.op(eng, lambda e, cvx=cvx, u=u, w=w, c=c: e.scalar_tensor_tensor(
                                out=cvx[:, 0:w], in0=u[:, 1:w + 1], scalar=cw(c, 1), in1=cvx[:, 0:w], op0=ALU.mult, op1=ALU.add),
                                reads=[Bu, Bcvx, self.Bcst], writes=[Bcvx])
                            S.op(eng, lambda e, cvx=cvx, u=u, w=w, c=c: e.scalar_tensor_tensor(
                                out=cvx[:, 0:w], in0=u[:, 2:w + 2], scalar=cw(c, 2), in1=cvx[:, 0:w], op0=ALU.mult, op1=ALU.add),
                                reads=[Bu, Bcvx, self.Bcst], writes=[Bcvx])
                        S.op("act", lambda e, r=r, w=w, ja=ja: e.activation(out=sg[r][:, 0:w], in_=cvg[r][:, 0:w], func=AF.Silu,
                                                                            bias=cb(44 + ja), scale=1.0),
                             reads=[Bcvg[r], self.Bcst], writes=[Bsg[r]])
                        tl2 = sorted(set(self.tt_of(x) for x in (c0, c0 + w - 1)))
                        S.op("dve", lambda e, r=r, w=w, ja=ja, jl=jl, c0=c0: e.scalar_tensor_tensor(
                            out=act[:, jl, c0:c0 + w], in0=cva[r][:, 0:w], scalar=cb(ja), in1=sg[r][:, 0:w], op0=ALU.add, op1=ALU.mult),
                            reads=[Bcva[r], Bsg[r], self.Bcst], writes=[Bact[(jl, t)] for t in tl2])
            plan([(blk * 4 + oc, t, t0, w) for blk in range(4) for oc in range(4) for t, (t0, w) in enumerate(TT)])
            idx = 0
            for blk in range(4):
                use("out")
                wb, Bw = self.wload(wo, layer * DFF + gstart * 128, JQ, blk * 512, 512)
                done("out")
                for oc in range(4):
                    kout = blk * 4 + oc
                    for t, (t0, w) in enumerate(TT):
                        pre(idx)
                        pst, Bps = self.ps()
                        for j in range(JQ):
                            S.op("pe", lambda e, wb=wb, oc=oc, j=j, t0=t0, w=w, pst=pst: e.matmul(
                                pst[:, 0:w], lhsT=wb[:, j, oc * 128:(oc + 1) * 128], rhs=act[:, j, t0:t0 + w],
                                start=(j == 0), stop=(j == JQ - 1)), reads=[Bw, Bact[(j, t)]], writes=[Bps])
                        post(idx, pst, Bps)
                        idx += 1
            gstart += JQ

    def phase_final(self):
        S, A = self.S, self.A
        o, n = CONST_LAYOUT["final_g"]
        xt = [A.take([16, 512], F32) for _ in range(2)]
        Bxt = [Buf(f"fxt{i}") for i in range(2)]
        sq = [A.take([16, 512], BF16) for _ in range(2)]
        Bsq = [Buf(f"fsq{i}") for i in range(2)]
        rs = [A.take([512], F32) for _ in range(2)]
        Brs = [Buf(f"frs{i}") for i in range(2)]
        yo = [A.take([D], F32) for _ in range(2)]
        Byo = [Buf(f"yo{i}") for i in range(2)]
        BO = Buf("OUT")
        oi = 0
        for t, (t0, w) in enumerate(TT[:4]):
            b = t % 2
            S.dma("sp", lambda e, b=b, t0=t0, w=w: e.dma_start(
                out=xt[b][:, :, 0:w], in_=self.xT[:, t0:t0 + w].rearrange("(k p) t -> p k t", p=128)),
                Bxt[b], reads=[self.BxT[(k, t)] for k in range(16)], writes=[Bxt[b]])
            S.op("act", lambda e, b=b, w=w: e.activation(out=sq[b][:, :, 0:w], in_=xt[b][:, :, 0:w], func=AF.Square),
                 reads=[Bxt[b]], writes=[Bsq[b]])
            pst, Bps = self.ps()
            for k in range(16):
                S.op("pe", lambda e, b=b, k=k, w=w, pst=pst: e.matmul(pst[:, 0:w], lhsT=self.ones[:, :], rhs=sq[b][:, k, 0:w],
                                                                      start=(k == 0), stop=(k == 15)),
                     reads=[Bsq[b], self.Bones], writes=[Bps])
            S.op("act", lambda e, b=b, w=w, pst=pst: e.activation(out=rs[b][:, 0:w], in_=pst[:, 0:w], func=AF.Ln,
                                                                  bias=self.epsb[:, 0:1], scale=1.0 / D),
                 reads=[Bps, self.Beps], writes=[Brs[b]])
            S.op("act", lambda e, b=b, w=w: e.activation(out=rs[b][:, 0:w], in_=rs[b][:, 0:w], func=AF.Exp, scale=-0.5),
                 reads=[Brs[b]], writes=[Brs[b]])
            for k in range(16):
                gcol = self.cst[:, o + k:o + k + 1]
                S.op("dve", lambda e, b=b, k=k, w=w, gcol=gcol: e.scalar_tensor_tensor(
                    out=xt[b][:, k, 0:w], in0=xt[b][:, k, 0:w], scalar=gcol, in1=rs[b][:, 0:w], op0=ALU.mult, op1=ALU.mult),
                    reads=[Bxt[b], Brs[b], self.Bcst], writes=[Bxt[b]])
            ev = S.dma("sp", lambda e, b=b, t0=t0, w=w: e.dma_start(
                out=self.out[:, t0:t0 + w].rearrange("(k p) t -> p k t", p=128), in_=xt[b][:, :, 0:w]),
                Bxt[b], reads=[Bxt[b]], writes=[Buf(f"OUT{t}")])
            self.final_events.append(ev)

    def dbg_sb(self, src, Bsrc, r0, npart):
        S, A = self.S, self.A
        f = A.take([2048], F32)
        Bf = Buf(f"dbgf{A.cnt}")
        A.cnt += 1
        S.op("dve", lambda e: e.tensor_copy(out=f[0:npart, :], in_=src), reads=[Bsrc] if Bsrc else [], writes=[Bf])
        ev = S.dma("sp", lambda e: e.dma_start(out=self.out[r0:r0 + npart, :], in_=f[0:npart, :]), Bf, reads=[Bf], writes=[Buf("o")])
        self.final_events.append(ev)
        self.dbg = True

    def dbg_rows(self, src_dram, r0, npart, dtype):
        S, A = self.S, self.A
        b = A.take([2048], dtype)
        Bb = Buf(f"dbgb{A.cnt}")
        A.cnt += 1
        S.dma("sp", lambda e: e.dma_start(out=b[0:npart, :], in_=src_dram), Bb, writes=[Bb])
        self.dbg_sb(b[0:npart, :], Bb, r0, npart)

    def stager(self, n, width, dtype=BF16):
        A = self.A
        bufs = [A.take([width], dtype) for _ in range(n)]
        Bs = [Buf(f"stg{i}_{A.cnt}") for i in range(n)]
        A.cnt += 1
        st = {"i": 0}

        def nxt():
            i = st["i"] % n
            st["i"] += 1
            return bufs[i], Bs[i]
        return nxt

    def evac_store(self, nxt, src, Bsrc, dst, dstB, npart=128, func=None, bias=None, Bbias=None):
        S = self.S
        stg, Bstg = nxt()
        w = src.shape[-1]
        o = stg[0:npart, 0:w]
        if func is not None:
            S.op("act", lambda e: e.activation(out=o, in_=src, func=func, bias=bias, scale=1.0),
                 reads=[Bsrc] + ([Bbias] if Bbias else []), writes=[Bstg])
        elif S.alt() == "act":
            S.op("act", lambda e: e.copy(out=o, in_=src), reads=[Bsrc], writes=[Bstg])
        else:
            S.op("dve", lambda e: e.tensor_copy(out=o, in_=src), reads=[Bsrc], writes=[Bstg])
        S.dma("sp", lambda e: e.dma_start(out=dst, in_=o), Bstg, reads=[Bstg], writes=dstB)

    def proj_fm(self, wb, Bw, oc, kc, src, srcB, epi):
        S = self.S
        for t, (t0, w) in enumerate(TT):
            pst, Bps = self.ps()
            for k in range(kc):
                S.op("pe", lambda e, k=k, t0=t0, w=w, pst=pst: e.matmul(
                    pst[:, 0:w], lhsT=wb[:, k, oc * 128:(oc + 1) * 128], rhs=src[:, k, t0:t0 + w],
                    start=(k == 0), stop=(k == kc - 1)), reads=[Bw, srcB(k, t)], writes=[Bps])
            epi(pst, Bps, t, t0, w)

    def proj_tm(self, wb, Bw, cb, kc, src, srcB, epi):
        S = self.S
        for n in range(NT // 128):
            t = self.tt_of(n * 128)
            pst, Bps = self.ps()
            for k in range(kc):
                S.op("pe", lambda e, k=k, n=n, pst=pst: e.matmul(
                    pst[:, 0:cb], lhsT=src[:, k, n * 128:(n + 1) * 128], rhs=wb[:, k, 0:cb],
                    start=(k == 0), stop=(k == kc - 1)), reads=[Bw, srcB(k, t)], writes=[Bps])
            epi(pst, Bps, n)

    def attn_finalize(self, o_ps, Bo, s_ps, Bs, w, dst, dstB, rsn):
        S = self.S
        rs, Brs = rsn()
        S.op("dve", lambda e: e.reciprocal(out=rs[:, 0:w], in_=s_ps[:, 0:w]), reads=[Bs], writes=[Brs])
        S.op("dve", lambda e: e.tensor_tensor(out=dst, in0=o_ps[:, 0:w], in1=rs[:, 0:w], op=ALU.mult),
             reads=[Bo, Brs], writes=[dstB])

    def phase_na(self, layer, j):
        S, A = self.S, self.A
        hB = lambda k, t: self.Bh[(k, t)]
        Bq = {(h, t): Buf(f"qd{h}_{t}") for h in range(16) for t in range(5)}
        Bk = {(h, t): Buf(f"kd{h}_{t}") for h in range(16) for t in range(5)}
        Bv = [Buf(f"vd{n}") for n in range(18)]
        self.wstream_init()
        nxt = self.stager(3, 512)
        w2d = self.I["a_w_qkv"]
        r0 = j * 2048
        for blk in range(8):
            wb, Bw = self.wload(w2d, r0, 16, blk * 512, 512)
            for oc in range(4):
                c = blk * 4 + oc
                h = c % 16
                dst, Bd = (self.qT, Bq) if c < 16 else (self.kT, Bk)
                self.proj_fm(wb, Bw, oc, 16, self.hT, hB,
                             lambda pst, Bps, t, t0, w, h=h, dst=dst, Bd=Bd: self.evac_store(
                                 nxt, pst[:, 0:w], Bps, dst[h, :, t0:t0 + w], [Bd[(h, t)]]))
        for blk in range(4):
            wb, Bw = self.wload(w2d, r0, 16, 4096 + blk * 512, 512)
            self.proj_tm(wb, Bw, 512, 16, self.hT, hB,
                         lambda pst, Bps, n, blk=blk: self.evac_store(
                             nxt, pst[:, 0:512], Bps, self.vd[n * 128:(n + 1) * 128, blk * 512:(blk + 1) * 512], [Bv[n]]))
        S.barrier()
        A.reset()
        self.psi = 0
        tab = A.take([16, 14, 64], BF16)
        Btab = Buf("natab")
        tst = [A.take([896], F32) for _ in range(2)]
        Btst = [Buf(f"tst{i}") for i in range(2)]
        for h in range(16):
            b = h % 2
            S.dma("sp", lambda e, h=h, b=b: e.dma_start(out=tst[b], in_=self.I["na_tab"][j * 128:(j + 1) * 128, h * 896:(h + 1) * 896]),
                  Btst[b], writes=[Btst[b]])
            S.op("act", lambda e, h=h, b=b: e.activation(out=tab[:, h, :, :].rearrange("p a b -> p (a b)"), in_=tst[b], func=AF.Exp),
                 reads=[Btst[b]], writes=[Btab])
        qh = [A.take([NT], BF16) for _ in range(2)]
        kh = [A.take([NT], BF16) for _ in range(2)]
        ve = [A.take([18, 128], BF16) for _ in range(2)]
        vo = [A.take([15, 128], BF16) for _ in range(2)]
        Bqh = [Buf(f"qh{i}") for i in range(2)]
        Bkh = [Buf(f"kh{i}") for i in range(2)]
        Bve = [Buf(f"ve{i}") for i in range(2)]
        Bvo = [Buf(f"vo{i}") for i in range(2)]
        enx = self.stager(5, 512)
        rsn = self.stager(2, 256, F32)
        scale = 128 ** -0.5
        for h in range(16):
            b = h % 2
            S.dma("sp", lambda e, h=h, b=b: e.dma_start(out=qh[b], in_=self.qT[h, :, :]), Bqh[b],
                  reads=[Bq[(h, t)] for t in range(5)], writes=[Bqh[b]])
            S.dma("sp", lambda e, h=h, b=b: e.dma_start(out=kh[b], in_=self.kT[h, :, :]), Bkh[b],
                  reads=[Bk[(h, t)] for t in range(5)], writes=[Bkh[b]])
            S.dma("sp", lambda e, h=h, b=b: e.dma_start(
                out=ve[b], in_=self.vd[:, h * 128:(h + 1) * 128].rearrange("(t p) d -> p t d", p=128)), Bve[b],
                reads=Bv, writes=[Bve[b]])
            S.dma("sp", lambda e, h=h, b=b: e.dma_start(
                out=vo[b], in_=self.vd[64:64 + 15 * 128, h * 128:(h + 1) * 128].rearrange("(t p) d -> p t d", p=128)), Bvo[b],
                reads=Bv, writes=[Bvo[b]])
            def row_S(r, h=h, b=b):
                sr = min(max(r - 4, 0), 24)
                ro0 = sr - r + 7
                q0 = 64 * r
                pst, Bps = self.ps(0, 4)
                ktiles = [(64 * sr + 128 * jt) for jt in range(4)] + [NL, NL + 128]
                for jt, k0 in enumerate(ktiles):
                    S.op("pe", lambda e, jt=jt, k0=k0: e.matmul(
                        pst[:, jt * 64:(jt + 1) * 64], lhsT=kh[b][:, k0:k0 + 128], rhs=qh[b][:, q0:q0 + 64],
                        start=True, stop=True), reads=[Bkh[b], Bqh[b]], writes=[Bps])
                ee, Be = enx()
                e3 = ee[:, 0:384].rearrange("p (a b) -> p a b", b=64)
                S.op("act", lambda e: e.activation(out=ee[:, 0:384], in_=pst[:, 0:384], func=AF.Exp, scale=scale),
                     reads=[Bps], writes=[Be])
                S.op("dve", lambda e: e.tensor_tensor(
                    out=e3[:, 0:4, :], in0=e3[:, 0:4, :], in1=tab[:, h, ro0:ro0 + 7:2, :], op=ALU.mult),
                    reads=[Be, Btab], writes=[Be])
                return (r, sr, q0, e3, Be)

            def row_PV(ctx, h=h, b=b):
                r, sr, q0, e3, Be = ctx
                o_ps, Bo = self.ps(4, 6)
                s_ps, Bs = self.ps(6, 8)
                for jt in range(6):
                    if jt < 4:
                        if sr % 2 == 0:
                            vt, Bvt = ve[b][:, sr // 2 + jt, :], Bve[b]
                        else:
                            vt, Bvt = vo[b][:, (sr - 1) // 2 + jt, :], Bvo[b]
                    else:
                        vt, Bvt = ve[b][:, 16 + (jt - 4), :], Bve[b]
                    S.op("pe", lambda e, vt=vt, jt=jt: e.matmul(
                        o_ps[:, 0:64], lhsT=vt, rhs=e3[:, jt, :], start=(jt == 0), stop=(jt == 5)),
                        reads=[Bvt, Be], writes=[Bo])
                for jt in range(6):
                    S.op("pe", lambda e, jt=jt: e.matmul(
                        s_ps[:, 0:64], lhsT=self.ones[:, :], rhs=e3[:, jt, :], start=(jt == 0), stop=(jt == 5)),
                        reads=[self.Bones, Be], writes=[Bs])
                self.attn_finalize(o_ps, Bo, s_ps, Bs, 64, self.hT[:, h, q0:q0 + 64], self.Bh[(h, self.tt_of(q0))], rsn)

            def ctx_S(h=h, b=b):
                pst, Bps = self.ps(0, 4)
                for jt in range(2):
                    S.op("pe", lambda e, jt=jt: e.matmul(
                        pst[:, jt * 256:(jt + 1) * 256], lhsT=kh[b][:, NL + jt * 128:NL + (jt + 1) * 128], rhs=qh[b][:, NL:NT],
                        start=True, stop=True), reads=[Bkh[b], Bqh[b]], writes=[Bps])
                ee, Be = enx()
                S.op("act", lambda e: e.activation(out=ee[:, 0:512], in_=pst[:, 0:512], func=AF.Exp, scale=scale),
                     reads=[Bps], writes=[Be])
                return (ee, Be)

            def ctx_PV(ctx, h=h, b=b):
                ee, Be = ctx
                o_ps, Bo = self.ps(4, 6)
                s_ps, Bs = self.ps(6, 8)
                for jt in range(2):
                    S.op("pe", lambda e, jt=jt: e.matmul(
                        o_ps[:, 0:256], lhsT=ve[b][:, 16 + jt, :], rhs=ee[:, jt * 256:(jt + 1) * 256], start=(jt == 0), stop=(jt == 1)),
                        reads=[Bve[b], Be], writes=[Bo])
                for jt in range(2):
                    S.op("pe", lambda e, jt=jt: e.matmul(
                        s_ps[:, 0:256], lhsT=self.ones[:, :], rhs=ee[:, jt * 256:(jt + 1) * 256], start=(jt == 0), stop=(jt == 1)),
                        reads=[self.Bones, Be], writes=[Bs])
                self.attn_finalize(o_ps, Bo, s_ps, Bs, 256, self.hT[:, h, NL:NT], self.Bh[(h, 4)], rsn)

            pend = []
            for r in range(32):
                pend.append(row_S(r))
                if len(pend) > 2:
                    row_PV(pend.pop(0))
            cctx = ctx_S()
            while pend:
                row_PV(pend.pop(0))
            ctx_PV(cctx)
        S.barrier()
        A.reset()
        self.psi = 0
        self.oproj(self.I["a_w_o"], j * 2048)

    def rms_rstd(self, pst, Bps, w, n, dst, Bdst):
        S = self.S
        S.op("act", lambda e: e.activation(out=dst[:, 0:w], in_=pst[:, 0:w], func=AF.Ln, bias=self.epsb[:, 0:1], scale=1.0 / n),
             reads=[Bps, self.Beps], writes=[Bdst])
        S.op("act", lambda e: e.activation(out=dst[:, 0:w], in_=dst[:, 0:w], func=AF.Exp, scale=-0.5), reads=[Bdst], writes=[Bdst])

    def phase_mla(self, layer):
        S, A = self.S, self.A
        hB = lambda k, t: self.Bh[(k, t)]
        Bq = {(h, t): Buf(f"mqd{h}_{t}") for h in range(16) for t in range(5)}
        Bk = {(h, t): Buf(f"mkd{h}_{t}") for h in range(16) for t in range(5)}
        Bqp = {(h, t): Buf(f"mqpd{h}_{t}") for h in range(16) for t in range(5)}
        Bv = [Buf(f"mvd{n}") for n in range(18)]
        cqn = A.take([4, NT], BF16)
        ckvn = A.take([4, NT], BF16)
        kpe = A.take([NT], BF16)
        rope = A.take([NT], F32)
        fold = A.take([128], BF16)
        Bfold = Buf("fold")
        S.dma("pool", lambda e: e.dma_start(out=fold, in_=self.I["fold"]), Bfold, writes=[Bfold])
        prod = [A.take([512], BF16) for _ in range(2)]
        Bprod = [Buf(f"prod{i}") for i in range(2)]
        pri = {"i": 0}

        def rope_fold(pz, Bpz, t0, w):
            i = pri["i"] % 2
            pri["i"] += 1
            S.op("dve", lambda e: e.tensor_tensor(out=prod[i][:, 0:w], in0=pz[:, 0:w], in1=rope[:, t0:t0 + w], op=ALU.mult),
                 reads=[Bpz, Brope], writes=[Bprod[i]])
            pf, Bpf = self.ps()
            S.op("pe", lambda e: e.matmul(pf[:, 0:w], lhsT=fold, rhs=prod[i][:, 0:w], start=True, stop=True),
                 reads=[Bfold, Bprod[i]], writes=[Bpf])
            return pf, Bpf
        Bcqn = {(c, t): Buf(f"cqn{c}_{t}") for c in range(4) for t in range(5)}
        Bckvn = {(c, t): Buf(f"ckvn{c}_{t}") for c in range(4) for t in range(5)}
        Bkpe = Buf("kpe")
        Brope = Buf("rope")
        S.dma("sp", lambda e: e.dma_start(out=rope, in_=self.I["rope"]), Brope, writes=[Brope])
        mark = A.mark()
        if os.environ.get("MLA_STOP") == "0":
            self.dbg_sb(self.hT[:, 0, 0:2048], None, 0, 128)
            self.dbg_sb(self.hT[:, 7, 0:2048], None, 128, 128)
            return
        wres = A.take([16, 1152], BF16)
        Bwres = [Buf(f"wres{i}") for i in range(3)]
        for i, (c0, cw) in enumerate(((0, 512), (512, 512), (1024, 128))):
            S.dma("pool", lambda e, c0=c0, cw=cw: e.dma_start(
                out=wres[:, :, c0:c0 + cw], in_=self.I["b_w_in"][:, c0:c0 + cw].rearrange("(k p) n -> p k n", p=128)),
                Bwres[i], writes=[Bwres[i]])
        craw = A.take([4, 512], F32)
        Bcraw = Buf("craw")
        sqb = A.take([4, 512], BF16)
        Bsqb = Buf("sqb")
        rq = A.take([512], F32)
        Brq = Buf("rq")
        t1 = A.take([512], F32)
        t2 = A.take([512], F32)
        Bt1, Bt2 = Buf("t1"), Buf("t2")
        oqn, _ = CONST_LAYOUT["q_norm"]
        okn, _ = CONST_LAYOUT["kv_norm"]
        for t, (t0, w) in enumerate(TT):
            for half, (dstn, Bdstn, on) in enumerate(((cqn, Bcqn, oqn), (ckvn, Bckvn, okn))):
                for c in range(4):
                    col = half * 512 + c * 128
                    pst, Bps = self.ps()
                    for k in range(16):
                        S.op("pe", lambda e, k=k, col=col, t0=t0, w=w, pst=pst: e.matmul(
                            pst[:, 0:w], lhsT=wres[:, k, col:col + 128], rhs=self.hT[:, k, t0:t0 + w],
                            start=(k == 0), stop=(k == 15)), reads=[Bwres[half], self.Bh[(k, t)]], writes=[Bps])
                    S.op("dve", lambda e, c=c, w=w, pst=pst: e.tensor_copy(out=craw[:, c, 0:w], in_=pst[:, 0:w]),
                         reads=[Bps], writes=[Bcraw])
                S.op("act", lambda e, w=w: e.activation(out=sqb[:, :, 0:w], in_=craw[:, :, 0:w], func=AF.Square),
                     reads=[Bcraw], writes=[Bsqb])
                pst, Bps = self.ps()
                for c in range(4):
                    S.op("pe", lambda e, c=c, w=w, pst=pst: e.matmul(pst[:, 0:w], lhsT=self.ones[:, :], rhs=sqb[:, c, 0:w],
                                                                      start=(c == 0), stop=(c == 3)),
                         reads=[Bsqb, self.Bones], writes=[Bps])
                self.rms_rstd(pst, Bps, w, 512, rq, Brq)
                for c in range(4):
                    gcol = self.cst[:, on + c:on + c + 1]
                    S.op("dve", lambda e, c=c, w=w, t0=t0, dstn=dstn, gcol=gcol: e.scalar_tensor_tensor(
                        out=dstn[:, c, t0:t0 + w], in0=craw[:, c, 0:w], scalar=gcol, in1=rq[:, 0:w], op0=ALU.mult, op1=ALU.mult),
                        reads=[Bcraw, Brq, self.Bcst], writes=[Bdstn[(c, t)]])
            pz, Bpz = self.ps()
            for k in range(16):
                S.op("pe", lambda e, k=k, t0=t0, w=w, pz=pz: e.matmul(
                    pz[:, 0:w], lhsT=wres[:, k, 1024:1152], rhs=self.hT[:, k, t0:t0 + w],
                    start=(k == 0), stop=(k == 15)), reads=[Bwres[2], self.Bh[(k, t)]], writes=[Bpz])
            pf, Bpf = rope_fold(pz, Bpz, t0, w)
            S.op("act", lambda e, t0=t0, w=w, pf=pf: e.copy(out=kpe[:, t0:t0 + w], in_=pf[:, 0:w]), reads=[Bpf], writes=[Bkpe])
        S.barrier()
        A.reset(mark)
        self.psi = 0
        self.wstream_init(nbuf=3, kc=4, cb=512)
        nxt = self.stager(3, 512)
        t1 = A.take([512], F32)
        t2 = A.take([512], F32)
        Bt1, Bt2 = Buf("t1b"), Buf("t2b")
        cqB = lambda c, t: Bcqn[(c, t)]
        ckB = lambda c, t: Bckvn[(c, t)]
        for blk in range(4):
            wb, Bw = self.wload(self.I["b_w_uq_n"], 0, 4, blk * 512, 512)
            for oc in range(4):
                h = blk * 4 + oc
                self.proj_fm(wb, Bw, oc, 4, cqn, cqB, lambda pst, Bps, t, t0, w, h=h: self.evac_store(
                    nxt, pst[:, 0:w], Bps, self.qT[h, :, t0:t0 + w], [Bq[(h, t)]]))
        for blk in range(4):
            wb, Bw = self.wload(self.I["b_w_uk"], 0, 4, blk * 512, 512)
            for oc in range(4):
                h = blk * 4 + oc
                self.proj_fm(wb, Bw, oc, 4, ckvn, ckB, lambda pst, Bps, t, t0, w, h=h: self.evac_store(
                    nxt, pst[:, 0:w], Bps, self.kT[h, :, t0:t0 + w], [Bk[(h, t)]]))
        for blk in range(4):
            wb, Bw = self.wload(self.I["b_w_uv"], 0, 4, blk * 512, 512)
            self.proj_tm(wb, Bw, 512, 4, ckvn, ckB, lambda pst, Bps, n, blk=blk: self.evac_store(
                nxt, pst[:, 0:512], Bps, self.vd[n * 128:(n + 1) * 128, blk * 512:(blk + 1) * 512], [Bv[n]]))
        for blk in range(4):
            wb, Bw = self.wload(self.I["b_w_uq_r"], 0, 4, blk * 512, 512)
            for oc in range(4):
                h = blk * 4 + oc
                for t, (t0, w) in enumerate(TT):
                    pz, Bpz = self.ps()
                    for k in range(4):
                        S.op("pe", lambda e, k=k, oc=oc, t0=t0, w=w, pz=pz, wb=wb: e.matmul(
                            pz[:, 0:w], lhsT=wb[:, k, oc * 128:(oc + 1) * 128], rhs=cqn[:, k, t0:t0 + w],
                            start=(k == 0), stop=(k == 3)), reads=[Bw, Bcqn[(k, t)]], writes=[Bpz])
                    pf, Bpf = rope_fold(pz, Bpz, t0, w)
                    self.evac_store(nxt, pf[:, 0:w], Bpf, self.qpT[h, :, t0:t0 + w], [Bqp[(h, t)]])
        S.barrier()
        A.reset(mark)
        self.psi = 0
        if os.environ.get("MLA_STOP") == "2":
            self.dbg_rows(self.qT[0, :, 0:2048], 0, 128, BF16)
            self.dbg_rows(self.kT[0, :, 0:2048], 128, 128, BF16)
            self.dbg_rows(self.qpT[0, 0:64, 0:2048], 256, 64, BF16)
            self.dbg_rows(self.vd[0:128, :], 384, 128, BF16)
            self.dbg_sb(kpe[0:64, 0:2048], Bkpe, 512, 64)
            self.dbg_sb(cqn[:, 0, 0:2048], None, 640, 128)
            return
        qh = [A.take([NT], BF16) for _ in range(2)]
        qph = [A.take([NT], BF16) for _ in range(2)]
        kh = [A.take([NT], BF16) for _ in range(2)]
        ve = [A.take([18, 128], BF16) for _ in range(2)]
        Bqh = [Buf(f"mqh{i}") for i in range(2)]
        Bqph = [Buf(f"mqph{i}") for i in range(2)]
        Bkh = [Buf(f"mkh{i}") for i in range(2)]
        Bve = [Buf(f"mve{i}") for i in range(2)]
        enx = self.stager(4, 512)
        rsn = self.stager(2, 512, F32)
        scale = 192 ** -0.5
        for h in range(16):
            b = h % 2
            S.dma("sp", lambda e, h=h, b=b: e.dma_start(out=qh[b], in_=self.qT[h, :, :]), Bqh[b],
                  reads=[Bq[(h, t)] for t in range(5)], writes=[Bqh[b]])
            S.dma("sp", lambda e, h=h, b=b: e.dma_start(out=qph[b], in_=self.qpT[h, :, :]), Bqph[b],
                  reads=[Bqp[(h, t)] for t in range(5)], writes=[Bqph[b]])
            S.dma("sp", lambda e, h=h, b=b: e.dma_start(out=kh[b], in_=self.kT[h, :, :]), Bkh[b],
                  reads=[Bk[(h, t)] for t in range(5)], writes=[Bkh[b]])
            S.dma("sp", lambda e, h=h, b=b: e.dma_start(
                out=ve[b], in_=self.vd[:, h * 128:(h + 1) * 128].rearrange("(t p) d -> p t d", p=128)), Bve[b],
                reads=Bv, writes=[Bve[b]])
            for t, (t0, w) in enumerate(TT):
                kts = list(range(18)) if t0 < NL else [16, 17]
                o_ps, Bo = self.ps(4, 6)
                s_ps, Bs = self.ps(6, 8)
                n = len(kts)

                def pv(i, kt, ee, Be, b=b, w=w, o_ps=o_ps, s_ps=s_ps, Bo=Bo, Bs=Bs, n=n):
                    S.op("pe", lambda e: e.matmul(o_ps[:, 0:w], lhsT=ve[b][:, kt, :], rhs=ee[:, 0:w], start=(i == 0), stop=(i == n - 1)),
                         reads=[Bve[b], Be], writes=[Bo])
                    S.op("pe", lambda e: e.matmul(s_ps[:, 0:w], lhsT=self.ones[:, :], rhs=ee[:, 0:w], start=(i == 0), stop=(i == n - 1)),
                         reads=[self.Bones, Be], writes=[Bs])

                pend = []
                for i, kt in enumerate(kts):
                    pst, Bps = self.ps(0, 4)
                    S.op("pe", lambda e, b=b, kt=kt, t0=t0, w=w, pst=pst: e.matmul(
                        pst[:, 0:w], lhsT=kh[b][:, kt * 128:(kt + 1) * 128], rhs=qh[b][:, t0:t0 + w], start=True, stop=False),
                        reads=[Bkh[b], Bqh[b]], writes=[Bps])
                    S.op("pe", lambda e, b=b, kt=kt, t0=t0, w=w, pst=pst: e.matmul(
                        pst[:, 0:w], lhsT=kpe[:, kt * 128:(kt + 1) * 128], rhs=qph[b][:, t0:t0 + w], start=False, stop=True),
                        reads=[Bkpe, Bqph[b]], writes=[Bps])
                    ee, Be = enx()
                    S.op("act", lambda e, ee=ee, pst=pst, w=w: e.activation(out=ee[:, 0:w], in_=pst[:, 0:w], func=AF.Exp, scale=scale),
                         reads=[Bps], writes=[Be])
                    pend.append((i, kt, ee, Be))
                    if len(pend) > 2:
                        pv(*pend.pop(0))
                while pend:
                    pv(*pend.pop(0))
                self.attn_finalize(o_ps, Bo, s_ps, Bs, w, self.hT[:, h, t0:t0 + w], self.Bh[(h, t)], rsn)
        S.barrier()
        A.reset()
        self.psi = 0
        self.oproj(self.I["b_w_o"], 0)

    def phase_sgu(self, layer):
        S, A = self.S, self.A
        hB = lambda k, t: self.Bh[(k, t)]
        Bu = {(c, t): Buf(f"ud{c}_{t}") for c in range(16) for t in range(5)}
        Bv = [Buf(f"svd{n}") for n in range(18)]
        w2d = self.I["c_w_in"]
        obu, _ = CONST_LAYOUT["b_u"]
        self.wstream_init()
        nxt = self.stager(3, 512)
        for blk in range(4):
            wb, Bw = self.wload(w2d, 0, 16, blk * 512, 512)
            for oc in range(4):
                c = blk * 4 + oc
                bcol = self.cst[:, obu + c:obu + c + 1]
                self.proj_fm(wb, Bw, oc, 16, self.hT, hB, lambda pst, Bps, t, t0, w, c=c, bcol=bcol: self.evac_store(
                    nxt, pst[:, 0:w], Bps, self.uT[c * 128:(c + 1) * 128, t0:t0 + w], [Bu[(c, t)]],
                    func=AF.Gelu, bias=bcol, Bbias=self.Bcst))
        S.barrier()
        A.reset()
        self.psi = 0
        if os.environ.get("SGU_STOP") == "1":
            return
        wv = A.take([16, 2048], BF16)
        Bwv = [Buf(f"wv{i}") for i in range(4)]
        for i in range(4):
            S.dma("pool", lambda e, i=i: e.dma_start(
                out=wv[:, :, i * 512:(i + 1) * 512], in_=w2d[:, 2048 + i * 512:2048 + (i + 1) * 512].rearrange("(k p) n -> p k n", p=128)),
                Bwv[i], writes=[Bwv[i]])
        rows = A.take([6144], F32)
        Brows = Buf("rows")
        S.dma("sp", lambda e: e.dma_start(out=rows, in_=self.I["rows"][:, 0:6144].broadcast_to([128, 6144])), Brows, writes=[Brows])
        vraw = [A.take([2048], F32) for _ in range(2)]
        Bvraw = [Buf(f"vraw{i}") for i in range(2)]
        sqt = A.take([2048], F32)
        Bsqt = Buf("sqt")
        vob = [A.take([2048], BF16) for _ in range(2)]
        Bvob = [Buf(f"vob{i}") for i in range(2)]
        st = A.take([8], F32)
        Bst = Buf("st")
        for n in range(18):
            b = n % 2
            t = self.tt_of(n * 128)
            for blk in range(4):
                pst, Bps = self.ps()
                for k in range(16):
                    S.op("pe", lambda e, k=k, n=n, blk=blk, pst=pst: e.matmul(
                        pst[:, 0:512], lhsT=self.hT[:, k, n * 128:(n + 1) * 128], rhs=wv[:, k, blk * 512:(blk + 1) * 512],
                        start=(k == 0), stop=(k == 15)), reads=[Bwv[blk], self.Bh[(k, t)]], writes=[Bps])
                S.op("dve", lambda e, b=b, blk=blk, pst=pst: e.tensor_tensor(
                    out=vraw[b][:, blk * 512:(blk + 1) * 512], in0=pst[:, 0:512], in1=rows[:, blk * 512:(blk + 1) * 512], op=ALU.add),
                    reads=[Bps, Brows], writes=[Bvraw[b]])
            S.op("act", lambda e, b=b: e.activation(out=vraw[b], in_=vraw[b], func=AF.Gelu), reads=[Bvraw[b]], writes=[Bvraw[b]])
            S.op("dve", lambda e, b=b: e.reduce_sum(out=st[:, 0:1], in_=vraw[b], axis=AX.X), reads=[Bvraw[b]], writes=[Bst])
            S.op("act", lambda e, b=b: e.activation(out=sqt, in_=vraw[b], func=AF.Square), reads=[Bvraw[b]], writes=[Bsqt])
            S.op("dve", lambda e: e.reduce_sum(out=st[:, 1:2], in_=sqt, axis=AX.X), reads=[Bsqt], writes=[Bst])
            S.op("dve", lambda e: e.tensor_scalar_mul(out=st[:, 2:3], in0=st[:, 0:1], scalar1=1.0 / 2048), reads=[Bst], writes=[Bst])
            S.op("dve", lambda e: e.tensor_tensor(out=st[:, 3:4], in0=st[:, 2:3], in1=st[:, 2:3], op=ALU.mult), reads=[Bst], writes=[Bst])
            S.op("dve", lambda e: e.scalar_tensor_tensor(out=st[:, 4:5], in0=st[:, 1:2], scalar=1.0 / 2048, in1=st[:, 3:4],
                                                         op0=ALU.mult, op1=ALU.subtract), reads=[Bst], writes=[Bst])
            S.op("act", lambda e: e.activation(out=st[:, 5:6], in_=st[:, 4:5], func=AF.Ln, bias=self.epsb[:, 0:1], scale=1.0),
                 reads=[Bst, self.Beps], writes=[Bst])
            S.op("act", lambda e: e.activation(out=st[:, 5:6], in_=st[:, 5:6], func=AF.Exp, scale=-0.5), reads=[Bst], writes=[Bst])
            S.op("dve", lambda e, b=b: e.tensor_scalar(out=vraw[b], in0=vraw[b], scalar1=st[:, 2:3], scalar2=st[:, 5:6],
                                                       op0=ALU.subtract, op1=ALU.mult), reads=[Bvraw[b], Bst], writes=[Bvraw[b]])
            S.op("dve", lambda e, b=b: e.tensor_tensor(out=vraw[b], in0=vraw[b], in1=rows[:, 2048:4096], op=ALU.mult),
                 reads=[Bvraw[b], Brows], writes=[Bvraw[b]])
            S.op("dve", lambda e, b=b: e.tensor_tensor(out=vob[b], in0=vraw[b], in1=rows[:, 4096:6144], op=ALU.add),
                 reads=[Bvraw[b], Brows], writes=[Bvob[b]])
            S.dma("sp", lambda e, n=n, b=b: e.dma_start(out=self.vd[n * 128:(n + 1) * 128, :], in_=vob[b]), Bvob[b],
                  reads=[Bvob[b]], writes=[Bv[n]])
        S.barrier()
        A.reset()
        self.psi = 0
        if os.environ.get("SGU_STOP") == "2":
            return
        wsT = A.take([16, 128], BF16)
        BwsT = Buf("wsT")
        S.dma("pool", lambda e: e.dma_start(out=wsT, in_=self.I["c_wsT"].rearrange("k (g p) -> k g p", p=128)), BwsT, writes=[BwsT])
        bsb = A.take([2048], F32)
        Bbsb = Buf("bsb")
        S.dma("sp", lambda e: e.dma_start(out=bsb, in_=self.I["rows"][:, 6144:8192].broadcast_to([128, 2048])), Bbsb, writes=[Bbsb])
        vn = [A.take([2048], BF16) for _ in range(2)]
        Bvn = [Buf(f"vn{i}") for i in range(2)]
        ut = [A.take([16, 128], BF16) for _ in range(2)]
        But = [Buf(f"ut{i}") for i in range(2)]
        tmx = [A.take([512], F32) for _ in range(2)]
        Btmx = [Buf(f"tmx{i}") for i in range(2)]
        ti = 0
        for n in range(18):
            b = n % 2
            t = self.tt_of(n * 128)
            S.dma("sp", lambda e, n=n, b=b: e.dma_start(out=vn[b], in_=self.vd[n * 128:(n + 1) * 128, :]), Bvn[b],
                  reads=[Bv[n]], writes=[Bvn[b]])
            S.dma("sp", lambda e, n=n, b=b: e.dma_start(
                out=ut[b], in_=self.uT[:, n * 128:(n + 1) * 128].rearrange("(g p) t -> p g t", p=128)), But[b],
                reads=[Bu[(c, t)] for c in range(16)], writes=[But[b]])
            for G in range(4):
                pst, Bps = self.ps()
                for gg in range(4):
                    g = 4 * G + gg
                    S.op("pe", lambda e, b=b, g=g, gg=gg, pst=pst: e.matmul(
                        pst[:, gg * 128:(gg + 1) * 128], lhsT=vn[b][:, g * 128:(g + 1) * 128], rhs=wsT[:, g, :], start=True, stop=True),
                        reads=[Bvn[b], BwsT], writes=[Bps])
                tb = ti % 2
                ti += 1
                S.op("dve", lambda e, G=G, tb=tb, pst=pst: e.tensor_tensor(
                    out=tmx[tb], in0=pst[:, 0:512], in1=bsb[:, G * 512:(G + 1) * 512], op=ALU.add),
                    reads=[Bps, Bbsb], writes=[Btmx[tb]])
                S.op("dve", lambda e, G=G, tb=tb, b=b, n=n: e.tensor_tensor(
                    out=self.hT[:, 4 * G:4 * G + 4, n * 128:(n + 1) * 128], in0=tmx[tb].rearrange("p (a b) -> p a b", b=128),
                    in1=ut[b][:, 4 * G:4 * G + 4, :], op=ALU.mult),
                    reads=[Btmx[tb], But[b]], writes=[self.Bh[(4 * G + gg, t)] for gg in range(4)])
        S.barrier()
        A.reset()
        self.psi = 0
        if os.environ.get("SGU_STOP") == "3":
            return
        self.oproj(self.I["c_w_o"], 0)


def _eps_setup(B):
    nc, S = B.nc, B.S
    B.epsb = nc.alloc_sbuf_tensor("epsb", [128, 1], F32)
    B.Beps = Buf("eps")
    S.op("dve", lambda e: e.memset(B.epsb[:, :], EPS), writes=[B.Beps])


_orig_build = Builder.build


def _build(self):
    _eps_setup(self)
    _orig_build(self)


Builder.build = _build


def full_plan():
    plan = [("ada",)]
    for i in range(DEPTH):
        kind, j = i % 3, i // 3
        plan.append(("norm", i, 0))
        plan.append((("na", i, j), ("mla", i), ("sgu", i))[kind])
        plan.append(("ffn", i))
    plan.append(("final",))
    return plan


_CACHE = {}


def kernel(**inputs):
    plan = full_plan()
    key = "full"
    if key not in _CACHE:
        _CACHE[key] = Builder(plan).nc
    nc = _CACHE[key]
    active = [0, 1, 4, 5]
    per_b = [host_layout(inputs, b) for b in range(4)]
    zero = {k: np.zeros_like(v) for k, v in per_b[0].items()}
    in_maps = [zero] * 8
    in_maps = list(in_maps)
    for b, c in enumerate(active):
        in_maps[c] = per_b[b]
    res = run_bass_kernel_spmd(nc, in_maps, core_ids=list(range(8)))
    out = np.stack([np.ascontiguousarray(np.asarray(res.results[c]["out"], np.float32).T) for c in active], axis=0)
    return out
```
